# Optimizing a Trainium2 kernel written in Bass

```python
import math
import jax, jax.numpy as jnp
from jax import lax
import numpy as np

D_MODEL = 1024
BATCH = 16
SEQ = 256
DEPTH = 4
DEC_BATCH = 2
DEC_SEQ = 4096
PAST_LEN = 256

GRID_W = 64
MIX_W = D_MODEL // 2
N_BRANCH = 4
H_A = 4
DK_A = MIX_W // H_A
DV_A = MIX_W // H_A
W_B = MIX_W
LRU_BLOCKS = 8
LRU_BW = W_B // LRU_BLOCKS
LRU_C = 8.0
H_C = 4
DK_C = MIX_W // H_C
DV_C = MIX_W // H_C
H_D = 4
DK_D = MIX_W // H_D
DV_D = MIX_W // H_D
W_D = MIX_W
CONV_K = 4
CONV_LEFT = 2
CHUNK = 64
CHUNK_D = 16
ROPE_BASE = 10000.0
EPS = 1e-6
IN_SIZES = (MIX_W, MIX_W, MIX_W, MIX_W,
            MIX_W, MIX_W,
            MIX_W, MIX_W, MIX_W, MIX_W, 2 * H_C, 2 * H_C,
            MIX_W, 2 * MIX_W, MIX_W, MIX_W,
            N_BRANCH * D_MODEL)
IN_COLS = sum(IN_SIZES)

kernel_name = 'bidir_hybrid_retention_rglru_gdn_hgrn2_prefix_ctx'


def rms_norm(x, g):
    xf = x.astype(jnp.float32)
    y = xf * lax.rsqrt(jnp.mean(xf * xf, axis=-1, keepdims=True) + EPS)
    return (y * g.astype(jnp.float32)).astype(x.dtype)


def head_norm(x):
    return x * lax.rsqrt(jnp.mean(x * x, axis=-1, keepdims=True) + EPS)


def l2_norm(x):
    return x * lax.rsqrt(jnp.sum(x * x, axis=-1, keepdims=True) + EPS)


def flip(t):
    return jnp.flip(t, axis=1)


def conv_centred(x, w):
    n = x.shape[1]
    xp = jnp.pad(x, ((0, 0), (CONV_LEFT, CONV_K - 1 - CONV_LEFT), (0, 0)))
    return sum(xp[:, j:j + n] * w[j] for j in range(CONV_K))


def grid_rope(rows):
    n_freq = DK_A // 4
    inv = ROPE_BASE ** (-jnp.arange(n_freq, dtype=jnp.float32) / n_freq)
    r = jnp.repeat(jnp.arange(rows, dtype=jnp.float32), GRID_W)
    col = jnp.tile(jnp.arange(GRID_W, dtype=jnp.float32), rows)
    ang = jnp.concatenate([r[:, None] * inv, col[:, None] * inv], axis=-1)
    return jnp.cos(ang), jnp.sin(ang)


def apply_rope(x, cos, sin):
    half = x.shape[-1] // 2
    x1, x2 = x[..., :half], x[..., half:]
    cos, sin = cos[None, :, None, :], sin[None, :, None, :]
    return jnp.concatenate([x1 * cos - x2 * sin, x2 * cos + x1 * sin], axis=-1)


def chunk_state_scan(s0, chunk_decay, chunk_kv):
    def step(s, inp):
        dec, kv = inp
        return s * dec[..., None] + kv, s
    s_fin, s_prev = lax.scan(step, s0, (jnp.moveaxis(chunk_decay, 1, 0), jnp.moveaxis(chunk_kv, 1, 0)))
    return s_fin, jnp.moveaxis(s_prev, 0, 1)


def retention_dir(q, k, v, log_gamma, s0):
    b, n_tok, h, dk = q.shape
    dv = v.shape[-1]
    nc = n_tok // CHUNK
    q = q.reshape(b, nc, CHUNK, h, dk)
    k = k.reshape(b, nc, CHUNK, h, dk)
    v = v.reshape(b, nc, CHUNK, h, dv)
    pos = jnp.arange(CHUNK, dtype=jnp.float32)
    rel = pos[:, None] - pos[None, :]
    causal = rel >= 0
    dmat = jnp.where(causal[None], jnp.exp(jnp.where(causal, rel, 0.0)[None] * log_gamma[:, None, None]), 0.0)
    scores = jnp.einsum('bnihd,bnjhd->bnhij', q, k) * dmat
    o_intra = jnp.einsum('bnhij,bnjhe->bnihe', scores, v)
    q_dec = jnp.exp((pos + 1.0)[:, None] * log_gamma)
    k_dec = jnp.exp((CHUNK - 1.0 - pos)[:, None] * log_gamma)
    kv = jnp.einsum('bnjhd,jh,bnjhe->bnhde', k, k_dec, v)
    chunk_dec = jnp.broadcast_to(jnp.exp(CHUNK * log_gamma)[:, None], (b, nc, h, dk))
    s_fin, s_prev = chunk_state_scan(s0, chunk_dec, kv)
    o_inter = jnp.einsum('bnihd,ih,bnhde->bnihe', q, q_dec, s_prev)
    return (o_intra + o_inter).reshape(b, n_tok, h, dv), s_fin


def rglru_dir(x, gate_w, gate_b, lam, h0):
    b, n_tok, w = x.shape
    xb = x.reshape(b, n_tok, LRU_BLOCKS, LRU_BW)
    gates = jnp.einsum('bnkc,gkcd->gbnkd', xb, gate_w).reshape(2, b, n_tok, w) + gate_b[:, None, None, :]
    r = jax.nn.sigmoid(gates[0])
    i = jax.nn.sigmoid(gates[1])
    log_a = -LRU_C * jax.nn.softplus(-lam) * r
    a = jnp.exp(log_a)
    u = jnp.sqrt(jnp.maximum(-jnp.expm1(2.0 * log_a), 1e-12)) * (i * x)
    u = u.at[:, 0].add(a[:, 0] * h0)

    def combine(lhs, rhs):
        return (lhs[0] * rhs[0], rhs[0] * lhs[1] + rhs[1])

    _, hs = lax.associative_scan(combine, (a, u), axis=1)
    return hs, hs[:, -1]


def gated_delta_dir(q, k, v, g, beta, s0):
    b, n_tok, h, dk = q.shape
    dv = v.shape[-1]
    nc = n_tok // CHUNK
    q = (q * dk ** -0.5).reshape(b, nc, CHUNK, h, dk)
    k = k.reshape(b, nc, CHUNK, h, dk)
    v = v.reshape(b, nc, CHUNK, h, dv)
    g = g.reshape(b, nc, CHUNK, h)
    beta = beta.reshape(b, nc, CHUNK, h)
    gc = jnp.cumsum(g, axis=2)
    rel = gc[:, :, :, None, :] - gc[:, :, None, :, :]
    tri = jnp.tril(jnp.ones((CHUNK, CHUNK), bool))[:, :, None]
    strict = jnp.tril(jnp.ones((CHUNK, CHUNK), bool), -1)[:, :, None]
    gam = jnp.where(tri, jnp.exp(jnp.where(tri, rel, 0.0)), 0.0)
    kk = jnp.einsum('bnihd,bnjhd->bnijh', k, k)
    a_mat = jnp.where(strict, beta[:, :, :, None, :] * kk * gam, 0.0)
    lhs = jnp.moveaxis(a_mat, -1, 2) + jnp.eye(CHUNK, dtype=a_mat.dtype)
    rhs_v = jnp.moveaxis(v * beta[..., None], 3, 2)
    rhs_k = jnp.moveaxis(k * (beta * jnp.exp(gc))[..., None], 3, 2)
    u = lax.linalg.triangular_solve(lhs, rhs_v, left_side=True, lower=True, unit_diagonal=True)
    w = lax.linalg.triangular_solve(lhs, rhs_k, left_side=True, lower=True, unit_diagonal=True)
    attn = jnp.moveaxis(jnp.einsum('bnihd,bnjhd->bnijh', q, k) * gam, -1, 2)
    q_dec = jnp.moveaxis(q * jnp.exp(gc)[..., None], 3, 2)
    k_dec = jnp.moveaxis(k * jnp.exp(gc[:, :, -1:] - gc)[..., None], 3, 2)
    g_last = jnp.exp(gc[:, :, -1])

    def step(s, inp):
        u_c, w_c, attn_c, qd_c, kd_c, gl_c = inp
        v_new = u_c - jnp.einsum('bhcd,bhde->bhce', w_c, s)
        o_c = jnp.einsum('bhcd,bhde->bhce', qd_c, s) + jnp.einsum('bhij,bhje->bhie', attn_c, v_new)
        s = s * gl_c[..., None, None] + jnp.einsum('bhcd,bhce->bhde', kd_c, v_new)
        return s, o_c

    xs = (jnp.moveaxis(u, 1, 0), jnp.moveaxis(w, 1, 0), jnp.moveaxis(attn, 1, 0),
          jnp.moveaxis(q_dec, 1, 0), jnp.moveaxis(k_dec, 1, 0), jnp.moveaxis(g_last, 1, 0))
    s_fin, o = lax.scan(step, s0, xs)
    o = jnp.moveaxis(jnp.moveaxis(o, 0, 1), 2, 3).reshape(b, n_tok, h, dv)
    return o, s_fin


def hgrn2_dir(q, k, v, log_f, s0):
    b, n_tok, h, dk = q.shape
    dv = v.shape[-1]
    nc = n_tok // CHUNK_D
    q = (q * dk ** -0.5).reshape(b, nc, CHUNK_D, h, dk)
    k = k.reshape(b, nc, CHUNK_D, h, dk)
    v = v.reshape(b, nc, CHUNK_D, h, dv)
    bc = jnp.cumsum(log_f.reshape(b, nc, CHUNK_D, h, dk), axis=2)
    tri = jnp.tril(jnp.ones((CHUNK_D, CHUNK_D), bool))[:, :, None, None]
    rel = bc[:, :, :, None] - bc[:, :, None]
    dmat = jnp.where(tri, jnp.exp(jnp.where(tri, rel, 0.0)), 0.0)
    scores = jnp.einsum('bnihd,bnjhd,bnijhd->bnhij', q, k, dmat)
    o_intra = jnp.einsum('bnhij,bnjhe->bnihe', scores, v)
    kv = jnp.einsum('bnjhd,bnjhe->bnhde', k * jnp.exp(bc[:, :, -1:] - bc), v)
    s_fin, s_prev = chunk_state_scan(s0, jnp.exp(bc[:, :, -1]), kv)
    o_inter = jnp.einsum('bnihd,bnhde->bnihe', q * jnp.exp(bc), s_prev)
    return (o_intra + o_inter).reshape(b, n_tok, h, dv), s_fin


def trunk_layer(x, mod_vec, rope, init, l, p):
    f32 = jnp.float32
    bsz, n_tok, _ = x.shape
    shift, scale, gate = jnp.split(jax.nn.silu(mod_vec) @ p['w_mod'][l] + p['b_mod'][l], 3, axis=-1)
    h = rms_norm(x, p['norm_g'][l]) * (1 + scale) + shift
    proj = h @ p['w_in'][l]
    offs = np.cumsum(IN_SIZES)[:-1].tolist()
    (aq, ak, av, az, bx, bz, cq, ck, cv, cz, ca, cb, dq, df, di, dz, mg) = jnp.split(proj, offs, axis=-1)

    def heads(t, nh):
        return t.astype(f32).reshape(bsz, n_tok, nh, -1)

    ret0, lru0, gdn0, hgrn0 = (s.astype(f32) for s in init)

    q, k, v = heads(aq, H_A), heads(ak, H_A), heads(av, H_A)
    if rope is not None:
        q, k = apply_rope(q, *rope), apply_rope(k, *rope)
    k = k * DK_A ** -0.5
    log_gamma = jnp.log1p(-jnp.exp(p['ret_decay'][l].astype(f32)))
    o_f, sr_f = retention_dir(q, k, v, log_gamma[0], ret0[:, 0])
    o_b, sr_b = retention_dir(flip(q), flip(k), flip(v), log_gamma[1], ret0[:, 1])
    out_a = head_norm(o_f + flip(o_b)).reshape(bsz, n_tok, MIX_W) * jax.nn.silu(az.astype(f32))

    xb = conv_centred(bx.astype(f32), p['lru_conv_w'][l].astype(f32)) + p['lru_conv_b'][l].astype(f32)
    gw = p['lru_gate_w'][l].astype(f32)
    gb = p['lru_gate_b'][l].astype(f32)
    lam = p['lru_lambda'][l].astype(f32)
    h_f, hl_f = rglru_dir(xb, gw[0], gb[0], lam[0], lru0[:, 0])
    h_b, hl_b = rglru_dir(flip(xb), gw[1], gb[1], lam[1], lru0[:, 1])
    out_b = (h_f + flip(h_b)) * jax.nn.silu(bz.astype(f32))

    qkv = jax.nn.silu(conv_centred(jnp.concatenate([cq, ck, cv], axis=-1).astype(f32),
                                   p['gdn_conv_w'][l].astype(f32)))
    q, k, v = jnp.split(qkv, 3, axis=-1)
    q, k, v = l2_norm(heads(q, H_C)), l2_norm(heads(k, H_C)), heads(v, H_C)
    a_in = ca.astype(f32).reshape(bsz, n_tok, 2, H_C)
    beta = jax.nn.sigmoid(cb.astype(f32).reshape(bsz, n_tok, 2, H_C))
    g = -jnp.exp(p['gdn_a_log'][l].astype(f32)) * jax.nn.softplus(a_in + p['gdn_dt_bias'][l].astype(f32))
    o_f, sg_f = gated_delta_dir(q, k, v, g[:, :, 0], beta[:, :, 0], gdn0[:, 0])
    o_b, sg_b = gated_delta_dir(flip(q), flip(k), flip(v), flip(g[:, :, 1]), flip(beta[:, :, 1]), gdn0[:, 1])
    out_c = rms_norm(o_f + flip(o_b), p['gdn_norm_g'][l]).reshape(bsz, n_tok, MIX_W) * jax.nn.silu(cz.astype(f32))

    q = jax.nn.silu(heads(dq, H_D))
    vi = heads(di, H_D)
    zf = df.astype(f32).reshape(bsz, n_tok, 2, H_D, DK_D)
    lb = p['lower_bounds'][l].reshape(2, H_D, DK_D)
    k_in = (1.0 - lb) * jax.nn.sigmoid(-zf)
    f_gate = lb + (1.0 - lb) * jax.nn.sigmoid(zf)
    log_f = jnp.log(jnp.maximum(f_gate, 1e-30))
    o_f, sh_f = hgrn2_dir(q, k_in[:, :, 0], vi, log_f[:, :, 0], hgrn0[:, 0])
    o_b, sh_b = hgrn2_dir(flip(q), flip(k_in[:, :, 1]), flip(vi), flip(log_f[:, :, 1]), hgrn0[:, 1])
    out_d = rms_norm(o_f + flip(o_b), p['hgrn_norm_g'][l]).reshape(bsz, n_tok, MIX_W) * jax.nn.silu(dz.astype(f32))

    br = jnp.stack([out_a, out_b, out_c, out_d], axis=2).astype(x.dtype)
    yb = jnp.einsum('bnkw,kwd->bnkd', br, p['w_branch'][l])
    gates = jax.nn.sigmoid(mg.reshape(bsz, n_tok, N_BRANCH, D_MODEL))
    y = jnp.sum(gates * yb, axis=2) @ p['w_out'][l]
    x = x + gate * y
    states = (jnp.stack([sr_f, sr_b], axis=1), jnp.stack([hl_f, hl_b], axis=1),
              jnp.stack([sg_f, sg_b], axis=1), jnp.stack([sh_f, sh_b], axis=1))
    return x, states


def setup_inputs(seed: int = 0) -> dict:
    key = jax.random.key(seed)
    ks = jax.random.split(key, 32)
    f32 = jnp.float32

    def nrm(k, shape, s):
        return s * jax.random.normal(k, shape, f32)

    a0 = jax.random.uniform(ks[17], (DEPTH, 2, W_B), f32, 0.9, 0.999)
    s_root = a0 ** (1.0 / LRU_C)
    dt = jnp.exp(jax.random.uniform(ks[20], (DEPTH, 2, H_C), f32, math.log(1e-3), math.log(1e-1)))
    ret_base = math.log(2.0) * (-5.0 - jnp.arange(H_A, dtype=f32))
    return {
        'x_prompt': nrm(ks[0], (BATCH, SEQ, D_MODEL), 1.0),
        'x_sample': nrm(ks[1], (DEC_BATCH, DEC_SEQ, D_MODEL), 1.0),
        'state_ret': nrm(ks[2], (DEC_BATCH, DEPTH, 2, H_A, DK_A, DV_A), 0.5),
        'state_lru': nrm(ks[3], (DEC_BATCH, DEPTH, 2, W_B), 0.5),
        'state_gdn': nrm(ks[4], (DEC_BATCH, DEPTH, 2, H_C, DK_C, DV_C), 0.1),
        'state_hgrn': nrm(ks[5], (DEC_BATCH, DEPTH, 2, H_D, DK_D, DV_D), 0.5),
        'c': nrm(ks[6], (DEC_BATCH, D_MODEL), 1.0),
        'c_ctx': nrm(ks[7], (D_MODEL,), 1.0),
        'norm_g': 1.0 + nrm(ks[8], (DEPTH, D_MODEL), 0.02),
        'w_mod': nrm(ks[9], (DEPTH, D_MODEL, 3 * D_MODEL), 0.5 * D_MODEL ** -0.5),
        'b_mod': nrm(ks[10], (DEPTH, 3 * D_MODEL), 0.02),
        'w_in': nrm(ks[11], (DEPTH, D_MODEL, IN_COLS), D_MODEL ** -0.5),
        'ret_decay': ret_base + nrm(ks[12], (DEPTH, 2, H_A), 0.05),
        'lru_conv_w': nrm(ks[13], (DEPTH, CONV_K, W_B), CONV_K ** -0.5),
        'lru_conv_b': nrm(ks[14], (DEPTH, W_B), 0.02),
        'lru_gate_w': nrm(ks[15], (DEPTH, 2, 2, LRU_BLOCKS, LRU_BW, LRU_BW), LRU_BW ** -0.5),
        'lru_gate_b': nrm(ks[16], (DEPTH, 2, 2, W_B), 0.02),
        'lru_lambda': jnp.log(s_root) - jnp.log1p(-s_root),
        'gdn_conv_w': nrm(ks[18], (DEPTH, CONV_K, 3 * MIX_W), CONV_K ** -0.5),
        'gdn_a_log': jnp.log(jax.random.uniform(ks[19], (DEPTH, 2, H_C), f32, 1.0, 16.0)),
        'gdn_dt_bias': dt + jnp.log(-jnp.expm1(-dt)),
        'gdn_norm_g': 1.0 + nrm(ks[21], (DEPTH, DV_C), 0.02),
        'hgrn_lb': nrm(ks[22], (DEPTH, 2, W_D), 0.1),
        'hgrn_norm_g': 1.0 + nrm(ks[23], (DEPTH, DV_D), 0.02),
        'w_branch': nrm(ks[24], (DEPTH, N_BRANCH, MIX_W, D_MODEL), MIX_W ** -0.5),
        'w_out': nrm(ks[25], (DEPTH, D_MODEL, D_MODEL), D_MODEL ** -0.5),
        'final_norm_g': 1.0 + nrm(ks[26], (D_MODEL,), 0.02),
    }


def reference(x_prompt, x_sample, state_ret, state_lru, state_gdn, state_hgrn, c, c_ctx,
              norm_g, w_mod, b_mod, w_in, ret_decay, lru_conv_w, lru_conv_b, lru_gate_w,
              lru_gate_b, lru_lambda, gdn_conv_w, gdn_a_log, gdn_dt_bias, gdn_norm_g,
              hgrn_lb, hgrn_norm_g, w_branch, w_out, final_norm_g):
    f32 = jnp.float32
    lb_sm = jax.nn.softmax(hgrn_lb.astype(f32), axis=0)
    lower_bounds = jnp.cumsum(lb_sm, axis=0) - lb_sm[0]
    p = dict(norm_g=norm_g, w_mod=w_mod, b_mod=b_mod, w_in=w_in, ret_decay=ret_decay,
             lru_conv_w=lru_conv_w, lru_conv_b=lru_conv_b, lru_gate_w=lru_gate_w,
             lru_gate_b=lru_gate_b, lru_lambda=lru_lambda, gdn_conv_w=gdn_conv_w,
             gdn_a_log=gdn_a_log, gdn_dt_bias=gdn_dt_bias, gdn_norm_g=gdn_norm_g,
             lower_bounds=lower_bounds, hgrn_norm_g=hgrn_norm_g, w_branch=w_branch, w_out=w_out)

    bp = x_prompt.shape[0]
    zero_init = (jnp.zeros((bp, 2, H_A, DK_A, DV_A), f32), jnp.zeros((bp, 2, W_B), f32),
                 jnp.zeros((bp, 2, H_C, DK_C, DV_C), f32), jnp.zeros((bp, 2, H_D, DK_D, DV_D), f32))
    mod_ctx = c_ctx[None, None, :]
    xc = x_prompt
    ret_l, lru_l, gdn_l, hgrn_l = [], [], [], []
    for l in range(DEPTH):
        xc, (s_r, s_l, s_g, s_h) = trunk_layer(xc, mod_ctx, None, zero_init, l, p)
        ret_l.append(s_r)
        lru_l.append(s_l)
        gdn_l.append(s_g)
        hgrn_l.append(s_h)
    y_prompt = rms_norm(xc, final_norm_g)
    new_ret = jnp.stack(ret_l, axis=1)
    new_lru = jnp.stack(lru_l, axis=1)
    new_gdn = jnp.stack(gdn_l, axis=1)
    new_hgrn = jnp.stack(hgrn_l, axis=1)

    rows = x_sample.shape[1] // GRID_W
    rope = grid_rope(rows)
    mod_lat = c[:, None, :]
    xs = x_sample
    for l in range(DEPTH):
        init = (state_ret[:, l], state_lru[:, l], state_gdn[:, l], state_hgrn[:, l])
        xs, _ = trunk_layer(xs, mod_lat, rope, init, l, p)
    y_sample = rms_norm(xs, final_norm_g)
    return (y_prompt, y_sample, new_ret, new_lru, new_gdn, new_hgrn)
```

```python
import numpy as np
import ml_dtypes
import concourse.bass as bass
import concourse.mybir as mybir
from concourse.bass_utils import run_bass_kernel_spmd
from contextlib import ExitStack
GNT = 8

F32 = mybir.dt.float32
BF16 = mybir.dt.bfloat16
AF = mybir.ActivationFunctionType
ALU = mybir.AluOpType
_DTSZ = {F32: 4, BF16: 2}
EPS = 1e-6


class Cfg:
    def __init__(self, depth=4, dm=1024, p_len=256, s_total=4096, nps=2, n_cores=8, n_pbatch=16, n_sbatch=2):
        self.depth = depth; self.dm = dm; self.kc = dm // 128; self.mw = dm // 2; self.nh = self.mw // 128
        self.p_len = p_len; self.nps = nps; self.s_total = s_total; self.s_len = s_total // 4
        self.n_cores = n_cores; self.n_pbatch = n_pbatch; self.n_sbatch = n_sbatch
        self.S0 = nps * p_len
        self.T = self.S0 + self.s_len
        self.segs = [(i * p_len, (i + 1) * p_len) for i in range(nps)] + [(self.S0, self.T)]
        mw = self.mw; nh = self.nh
        sizes = [mw] * 4 + [mw] * 2 + [mw] * 4 + [2 * nh, 2 * nh] + [mw, 2 * mw, mw, mw] + [4 * dm]
        names = ['aq', 'ak', 'av', 'az', 'bx', 'bz', 'cq', 'ck', 'cv', 'cz', 'ca', 'cb', 'dq', 'df', 'di', 'dz', 'mg']
        offs = np.cumsum([0] + sizes)
        self.off = {n: int(o) for n, o in zip(names, offs[:-1])}
        self.in_cols = int(offs[-1])
        self.blocks = []
        for (a, b, g) in [(0, self.S0, 0), (self.S0, self.T, 1)]:
            t = a
            while t < b:
                n = min(512, b - t); self.blocks.append((t, n, g)); t += n
        self.n64 = self.T // 64
        self.n32 = self.T // 32
        self.groups = [list(range(g * 4, g * 4 + 4)) for g in range(n_cores // 4)]
        self.padoff = [a + 3 * i + 2 for i, (a, b) in enumerate(self.segs)]
        self.TP = self.T + 3 * len(self.segs)
        L = {}
        o = 0
        def add(n, w):
            nonlocal o
            L[n] = (o, w); o += w
        kc = self.kc
        for l in range(depth):
            add('normg%d' % l, kc); add('bmod%d' % l, 3 * kc); add('retd%d' % l, 2 * nh)
            add('lcw%d' % l, 4 * nh); add('lcb%d' % l, nh); add('lgb%d' % l, 4 * nh); add('llam%d' % l, 2 * nh)
            add('gcw%d' % l, 12 * nh); add('galog%d' % l, 2 * nh); add('gdt%d' % l, 2 * nh)
            add('gng%d' % l, 1); add('hng%d' % l, 1)
        add('hlb', depth * 2 * nh); add('fng', kc); add('masks', 16); add('slru', depth * 2 * nh)
        self.sl = L; self.ns = o
        C = {}
        o = 0
        for n, w in [('ident', 128), ('mgf', 64), ('mgb', 64), ('trif', 64), ('trib', 64), ('negtf', 64), ('negtb', 64), ('poslf', 64), ('poslb', 64)]:
            C[n] = (o, w); o += w
        self.cl = C; self.ncst = o


def _region(ap):
    t = ap.tensor
    sz = _DTSZ.get(ap.dtype, 4)
    pat = ap.ap
    off = ap.offset
    space = str(ap.space)
    if space == "PSUM":
        return (t.name, 0, 128, 0, 1 << 30)
    lo = 0; hi = 0
    if space == "SB":
        pstep, pcnt = pat[0]
        for st, cn in pat[1:]:
            if st >= 0: hi += st * (cn - 1)
            else: lo += st * (cn - 1)
        if pstep > 0:
            p0 = off // pstep; f0 = off % pstep
        else:
            p0 = 0; f0 = off
        return (t.name, p0, p0 + pcnt, (f0 + lo) * sz, (f0 + hi + 1) * sz)
    for st, cn in pat:
        if st >= 0: hi += st * (cn - 1)
        else: lo += st * (cn - 1)
    return (t.name, 0, 1, (off + lo) * sz, (off + hi + 1) * sz)


class Op:
    __slots__ = ("eng", "fn", "deps", "signal", "isdma", "sem", "semval", "idx", "phase", "n", "dt", "coll")


class Sched:
    ENGS = ("pe", "act", "dve", "pool", "sp")

    def __init__(self, nc, n_dma_sems=16):
        self.nc = nc
        self.ops = {e: [] for e in self.ENGS}
        self.hist = {}
        self.n_dma_sems = n_dma_sems
        self.all_ops = []

    def op(self, eng, fn, reads=(), writes=(), dma=False, coll=False):
        o = Op()
        o.eng = eng; o.fn = fn; o.signal = False; o.isdma = dma; o.idx = len(self.all_ops); o.coll = coll
        o.phase = getattr(self, "phase", ""); o.n = 0; o.dt = None
        if writes:
            w0 = writes[0]; n = 1
            for st_, cn_ in w0.ap[1:]: n *= cn_
            o.n = n; o.dt = (reads[0].dtype if reads else None)
        deps = []
        rr = [_region(a) for a in reads]
        wr = [_region(a) for a in writes]
        for (name, p0, p1, f0, f1) in rr:
            for rec in self.hist.get(name, ()):
                if rec[0] < p1 and p0 < rec[1] and rec[2] < f1 and f0 < rec[3] and rec[4] is not None:
                    deps.append(rec[4])
        for (name, p0, p1, f0, f1) in wr:
            for rec in self.hist.get(name, ()):
                if rec[0] < p1 and p0 < rec[1] and rec[2] < f1 and f0 < rec[3]:
                    if rec[4] is not None: deps.append(rec[4])
                    deps.extend(rec[5])
        seen = set(); dd = []
        for d in deps:
            if id(d) not in seen:
                seen.add(id(d)); dd.append(d)
        o.deps = dd
        for (name, p0, p1, f0, f1) in rr:
            lst = self.hist.setdefault(name, [])
            found = False
            for rec in lst:
                if rec[0] < p1 and p0 < rec[1] and rec[2] < f1 and f0 < rec[3]:
                    rec[5].append(o); found = True
            if not found:
                lst.append([p0, p1, f0, f1, None, [o]])
        for (name, p0, p1, f0, f1) in wr:
            lst = self.hist.setdefault(name, [])
            keep = [rec for rec in lst if not (rec[0] >= p0 and rec[1] <= p1 and rec[2] >= f0 and rec[3] <= f1)]
            keep.append([p0, p1, f0, f1, o, []])
            self.hist[name] = keep
        self.ops[eng].append(o)
        self.all_ops.append(o)
        return o

    def emit(self, block, st, final_wait_ops=()):
        nc = self.nc
        for o in self.all_ops:
            for d in o.deps:
                d.signal = True
        for o in final_wait_ops:
            o.signal = True
        esem = {e: st.enter_context(nc.semaphore("s_" + e)) for e in self.ENGS}
        dsem = {e: [st.enter_context(nc.semaphore("d_%s_%d" % (e, i))) for i in range(self.n_dma_sems)] for e in ("sp",)}
        ecnt = {e: 0 for e in self.ENGS}
        dcnt = {e: [0] * self.n_dma_sems for e in dsem}
        dlast = {e: [None] * self.n_dma_sems for e in dsem}
        drr = {e: 0 for e in dsem}
        for o in self.all_ops:
            if o.isdma:
                k = drr[o.eng]; drr[o.eng] = (k + 1) % self.n_dma_sems
                prev = dlast[o.eng][k]
                if prev is not None:
                    o.deps.append(prev)
                dcnt[o.eng][k] += 16
                o.sem = dsem[o.eng][k]; o.semval = dcnt[o.eng][k]
                dlast[o.eng][k] = o
                o.signal = True
            elif o.coll:
                o.sem = st.enter_context(nc.semaphore("c_%d" % o.idx)); o.semval = 1
                o.signal = True
            elif o.signal:
                ecnt[o.eng] += 1
                o.sem = esem[o.eng]; o.semval = ecnt[o.eng]
        self.ecnt = ecnt

        def run_engine(ename, h, extra_wait):
            waited = {}
            for o in self.ops[ename]:
                for d in o.deps:
                    if ename == "pe" and (not d.isdma) and (not d.coll) and d.eng == "pe":
                        continue
                    key = d.sem.name
                    if waited.get(key, 0) >= d.semval:
                        continue
                    waited[key] = d.semval
                    h.wait_ge(d.sem, d.semval)
                ins = o.fn(h)
                if o.signal:
                    ins.then_inc(o.sem, 16 if o.isdma else 1)
            for d in extra_wait:
                key = d.sem.name
                if waited.get(key, 0) >= d.semval: continue
                waited[key] = d.semval
                h.wait_ge(d.sem, d.semval)

        @block.tensor
        def _(h): run_engine("pe", h, ())

        @block.scalar
        def _(h): run_engine("act", h, ())

        @block.vector
        def _(h): run_engine("dve", h, ())

        @block.gpsimd
        def _(h): run_engine("pool", h, ())

        @block.sync
        def _(h): run_engine("sp", h, list(final_wait_ops))


def build_program(cfg, dbg=None):
    nc = bass.Bass("TRN2", target_bir_lowering=False)
    depth, KC, NH, T, S0, SL = cfg.depth, cfg.kc, cfg.nh, cfg.T, cfg.S0, cfg.s_len
    DM, MW = cfg.dm, cfg.mw
    n64, n32 = cfg.n64, cfg.n32
    off = cfg.off
    NSEQ = cfg.nps

    def din(name, shape):
        return nc.dram_tensor(name, list(shape), F32, kind="ExternalInput").ap()

    def dout(name, shape):
        return nc.dram_tensor(name, list(shape), F32, kind="ExternalOutput").ap()

    xT_in = din("xT", [DM, T]); cT_in = din("cT", [128, KC * 2]); smalls_in = din("smalls", [128, cfg.ns])
    cst_in = din("cst", [128, cfg.ncst]); rm_in = din("rm", [128, 2 * T]); rope_in = din("rope", [128, 2 * SL])
    w_in = din("w_in", [depth, DM, cfg.in_cols]); w_mod = din("w_mod", [depth, DM, 3 * DM])
    w_branch = din("w_branch", [depth, 4, MW, DM]); w_out = din("w_out", [depth, DM, DM])
    lrugw_in = din("lrugw", [depth, 128, 4 * NH * 128])
    s0_ret = din("s0_ret", [depth, 2, NH, 128, 128]); s0_gdn = din("s0_gdn", [depth, 2, NH, 128, 128])
    s0_hgrn = din("s0_hgrn", [depth, 2, NH, 128, 128])
    yT_out = dout("yT", [DM, T])
    o_ret = dout("o_ret", [NSEQ, depth, 2, NH, 128, 128]); o_gdn = dout("o_gdn", [NSEQ, depth, 2, NH, 128, 128])
    o_hgrn = dout("o_hgrn", [NSEQ, depth, 2, NH, 128, 128]); o_lru = dout("o_lru", [128, NSEQ * depth * 2 * NH])
    dbg_out = {}
    xscr = [nc.dram_tensor("xscr%d" % i, [DM, T], F32).ap() for i in range(2)]
    spill = nc.dram_tensor("spill", [NH, 3, 128, 2 * SL], F32).ap()

    S = Sched(nc)
    fin = []
    with ExitStack() as st:
        def sb(name, shape, dt=F32):
            return st.enter_context(nc.sbuf_tensor("sb_" + name, list(shape), dt))

        def dma(out, in_, slow=False):
            if slow:
                return S.op("sp", lambda h: h.dma_start(out=out, in_=in_, allow_slow_non_contiguous=True), reads=[in_], writes=[out], dma=True)
            return S.op("sp", lambda h: h.dma_start(out=out, in_=in_), reads=[in_], writes=[out], dma=True)

        def mm(out, lhsT, rhs, start=True, stop=True):
            return S.op("pe", lambda h: h.matmul(out, lhsT=lhsT, rhs=rhs, start=start, stop=stop), reads=[lhsT, rhs], writes=[out])

        def tr(out, in_, idn):
            return S.op("pe", lambda h: h.transpose(out, in_, idn), reads=[in_, idn], writes=[out])

        SPLIT = 512

        def _splitF(*aps):
            F = None
            for a in aps:
                if len(a.ap) != 2: return None
                st_, n_ = a.ap[1]
                if st_ != 1: return None
                if F is None: F = n_
                elif F != n_: return None
            return F if (F is not None and F >= 2 * SPLIT and F % SPLIT == 0) else None

        def _blocks(F):
            return [(c, c + SPLIT) for c in range(0, F, SPLIT)]

        def act(out, in_, func, bias=None, scale=None):
            F = _splitF(out, in_)
            if F:
                for (c0, c1) in _blocks(F):
                    act(out[:, c0:c1], in_[:, c0:c1], func, bias=bias, scale=scale)
                return
            kw = {}
            rds = [in_]
            if bias is not None:
                kw["bias"] = bias
                if not isinstance(bias, float): rds.append(bias)
            if scale is not None:
                kw["scale"] = scale
                if not isinstance(scale, float): rds.append(scale)
            return S.op("act", lambda h: h.activation(out=out, in_=in_, func=func, **kw), reads=rds, writes=[out])

        def tt(out, in0, in1, op, e="dve"):
            F = _splitF(out, in0, in1)
            if F:
                for (c0, c1) in _blocks(F):
                    tt(out[:, c0:c1], in0[:, c0:c1], in1[:, c0:c1], op, e=e)
                return
            return S.op(e, lambda h: h.tensor_tensor(out=out, in0=in0, in1=in1, op=op), reads=[in0, in1], writes=[out])

        def ts(out, in0, s1, s2, op0, op1=None, e="dve"):
            F = _splitF(out, in0)
            if F:
                for (c0, c1) in _blocks(F):
                    ts(out[:, c0:c1], in0[:, c0:c1], s1, s2, op0, op1, e=e)
                return
            rds = [in0] + [s_ for s_ in (s1, s2) if s_ is not None and not isinstance(s_, float)]
            if op1 is None:
                return S.op(e, lambda h: h.tensor_scalar(out=out, in0=in0, scalar1=s1, scalar2=None, op0=op0), reads=rds, writes=[out])
            return S.op(e, lambda h: h.tensor_scalar(out=out, in0=in0, scalar1=s1, scalar2=s2, op0=op0, op1=op1), reads=rds, writes=[out])

        def stt(out, in0, scalar, in1, op0, op1, e="dve"):
            F = _splitF(out, in0, in1)
            if F:
                for (c0, c1) in _blocks(F):
                    stt(out[:, c0:c1], in0[:, c0:c1], scalar, in1[:, c0:c1], op0, op1, e=e)
                return
            rds = [in0, in1] + ([] if isinstance(scalar, float) else [scalar])
            return S.op(e, lambda h: h.scalar_tensor_tensor(out=out, in0=in0, scalar=scalar, in1=in1, op0=op0, op1=op1), reads=rds, writes=[out])

        def cpy(out, in_, e="pool"):
            if e == "act":
                return act(out, in_, AF.Copy)
            F = _splitF(out, in_)
            if F:
                for (c0, c1) in _blocks(F):
                    cpy(out[:, c0:c1], in_[:, c0:c1], e=e)
                return
            return S.op(e, lambda h: h.tensor_copy(out=out, in_=in_), reads=[in_], writes=[out])

        def memset(ap, val, e="pool"):
            return S.op(e, lambda h: h.memset(ap, val), writes=[ap])

        def recip(out, in_):
            return S.op("dve", lambda h: h.reciprocal(out=out, in_=in_), reads=[in_], writes=[out])

        def scan(out, d0, d1, init, rev=False):
            if rev:
                def r(a):
                    pstep, pcnt = a.ap[0]
                    (st1, n) = a.ap[1]
                    assert len(a.ap) == 2 and st1 == 1
                    return bass.AP(a.tensor, a.offset + n - 1, [[pstep, pcnt], [-1, n]])
                o2, a2, b2 = r(out), r(d0), r(d1)
            else:
                o2, a2, b2 = out, d0, d1
            return S.op("dve", lambda h: h.tensor_tensor_scan(out=o2, data0=a2, data1=b2, initial=init, op0=ALU.mult, op1=ALU.add),
                        reads=[d0, d1], writes=[out])

        hT = sb("hT", [128, KC, T], BF16)
        ysum = sb("ysum", [128, KC, T], BF16)
        outT = sb("outT", [128, NH, T], BF16)
        wst = [sb("wst%d" % i, [128, KC, 256]) for i in range(2)]
        wbf = [sb("wbf%d" % i, [128, KC, 256], BF16) for i in range(2)]
        cst = sb("cst", [128, cfg.ncst]); smalls = sb("smalls", [128, cfg.ns])
        rmb = sb("rmb", [128, 2 * T], BF16); rope = sb("rope", [128, 2 * SL])
        ones1 = sb("ones1", [128, 128]); onesdm = sb("onesdm", [128, 128]); ones128 = sb("ones128", [128, 128])
        onesT = sb("onesT", [128, SL], BF16); zerosT = sb("zerosT", [128, SL], BF16)
        epsc = sb("epsc", [128, 1])
        identb = sb("identb", [128, 128], BF16)
        cs = sb("cs", [128, KC * 2]); modv = sb("modv", [128, depth * 3 * KC * 2]); a1v = sb("a1v", [128, depth * KC * 2])
        lbv = sb("lbv", [128, depth * 2 * NH]); omlv = sb("omlv", [128, depth * 2 * NH]); nomlv = sb("nomlv", [128, depth * 2 * NH])
        halo = sb("halo", [128, 4 * NH * 3])
        ARENA_W = 20480
        arena = sb("arena", [128, ARENA_W])
        ps = [st.enter_context(nc.psum_tensor("ps%d" % i, [128, 512], F32)) for i in range(8)]
        psi = [0]

        def PS():
            p = ps[psi[0] % 8]; psi[0] += 1
            return p

        MT0 = ARENA_W - 1280

        class Arena:
            def __init__(self): self.o = 0
            def f32(self, n):
                a = arena[:, self.o:self.o + n]; self.o += n
                assert self.o <= MT0, self.o
                return a
            def bf16(self, n):
                w = (n + 1) // 2
                a = arena[:, self.o:self.o + w].bitcast(BF16); self.o += w
                assert self.o <= MT0, self.o
                return a
        Wb_m = arena[:, MT0:MT0 + 256].bitcast(BF16)[:, 0:NH * 128].rearrange("p (k c) -> p k c", c=128)
        Wg_m = arena[:, MT0 + 256:MT0 + 768].bitcast(BF16)[:, 0:KC * 128].rearrange("p (k c) -> p k c", c=128)
        gs_m = arena[:, MT0 + 768:MT0 + 1024].bitcast(BF16)
        tm_m = arena[:, MT0 + 1024:MT0 + 1280].bitcast(BF16)
        bg = []

        def pump(n=1):
            return

        def drain():
            while bg:
                try:
                    next(bg[0])
                except StopIteration:
                    bg.pop(0)

        def sm(name, i=0, w=1):
            o, _ = cfg.sl[name]
            return smalls[:, o + i:o + i + w]

        def cc(name, rows=128, w=None):
            o, ww = cfg.cl[name]
            return cst[0:rows, o:o + (w or ww)]

        ident = cc('ident')
        wrr = [0, 0]

        def wload(src2d, nk, ncols, cast=True):
            i = wrr[0] % 2; wrr[0] += 1
            dma(wst[i][:, :nk, :ncols], src2d.rearrange("(k p) c -> p k c", p=128))
            if not cast:
                return wst[i]
            j = wrr[1] % 2; wrr[1] += 1
            for k0 in range(0, nk, 2):
                k1 = min(nk, k0 + 2)
                cpy(wbf[j][:, k0:k1, :ncols], wst[i][:, k0:k1, :ncols], e=("act" if (k0 // 2) % 2 == 0 else "pool"))
            return wbf[j]

        dma(cst[:], cst_in); dma(smalls[:], smalls_in); dma(rope[:], rope_in)
        ar = Arena()
        rmf = ar.f32(2 * T)
        dma(rmf, rm_in)
        cpy(rmb[:], rmf, e="pool")
        memset(ones1[:], 1.0); memset(onesdm[:], 1.0 / DM); memset(ones128[:], 1.0 / 128.0)
        memset(onesT[:], 1.0); memset(zerosT[:], 0.0); memset(epsc[:], EPS)
        memset(halo[:], 0.0)
        cpy(identb[:], cc('ident'), e="pool")
        dma(cs[:], cT_in)
        act(cs[:], cs[:], AF.Silu)
        W2 = 2 * NH
        ee = sb("ee", [128, depth * W2]); ssum = sb("ssum", [128, W2])
        act(ee[:], sm('hlb', 0, depth * W2), AF.Exp)
        cpy(ssum[:], ee[:, 0:W2], e="dve")
        for l in range(1, depth):
            tt(ssum[:], ssum[:], ee[:, l * W2:(l + 1) * W2], ALU.add)
        recip(ssum[:], ssum[:])
        memset(lbv[:, 0:W2], 0.0, e="dve")
        for l in range(1, depth):
            tt(ee[:, l * W2:(l + 1) * W2], ee[:, l * W2:(l + 1) * W2], ssum[:], ALU.mult)
            tt(lbv[:, l * W2:(l + 1) * W2], lbv[:, (l - 1) * W2:l * W2], ee[:, l * W2:(l + 1) * W2], ALU.add)
        ts(omlv[:], lbv[:], -1.0, 1.0, ALU.mult, ALU.add)
        ts(nomlv[:], omlv[:], -1.0, None, ALU.mult)

        def mod_layer_gen(l):
            for ct in range(3 * KC):
                W = wload(w_mod[l, :, ct * 128:(ct + 1) * 128], KC, 128, cast=False)
                p = PS()
                for k in range(KC):
                    mm(p[:, 0:2], W[:, k, 0:128], cs[:, 2 * k:2 * k + 2], start=(k == 0), stop=(k == KC - 1))
                o = (l * 3 * KC + ct) * 2
                ts(modv[:, o:o + 2], p[:, 0:2], sm('bmod%d' % l, ct), None, ALU.add)
                yield
            for k in range(KC):
                o = (l * 3 * KC + KC + k) * 2
                stt(a1v[:, (l * KC + k) * 2:(l * KC + k) * 2 + 2], modv[:, o:o + 2], 1.0,
                    sm('normg%d' % l, k).to_broadcast([128, 2]), ALU.add, ALU.mult)
            yield

        def mod_layer(l):
            for _ in mod_layer_gen(l):
                pass

        def shiftc(l, k, g): return modv[:, (l * 3 * KC + k) * 2 + g:(l * 3 * KC + k) * 2 + g + 1]
        def gatec(l, k, g): return modv[:, (l * 3 * KC + 2 * KC + k) * 2 + g:(l * 3 * KC + 2 * KC + k) * 2 + g + 1]
        def a1c(l, k, g): return a1v[:, (l * KC + k) * 2 + g:(l * KC + k) * 2 + g + 1]

        def norm_block(xb, sq, b0, n, g, l):
            for k in range(KC):
                act(sq[:, k, :n], xb[:, k, :n], AF.Square)
            p = PS()
            for k in range(KC):
                mm(p[:, :n], onesdm[:], sq[:, k, :n], start=(k == 0), stop=(k == KC - 1))
            rs = sq[:, 0, :n]
            act(rs, p[:, :n], AF.Sqrt, bias=epsc[:])
            recip(rs, rs)
            for k in range(KC):
                tmp = sq[:, 1 + (k % 2), :n] if KC > 2 else sq[:, 1, :n]
                tt(tmp, xb[:, k, :n], rs, ALU.mult)
                if l < depth:
                    ts(hT[:, k, b0:b0 + n], tmp, a1c(l, k, g), shiftc(l, k, g), ALU.mult, ALU.add)
                else:
                    ts(xb[:, k, :n], tmp, sm('fng', k), None, ALU.mult)
            if l == depth:
                fin.append(dma(yT_out.rearrange("(k p) t -> p k t", p=128)[:, :, b0:b0 + n], xb[:, :, :n]))

        def proj_fm(l, col0, ncols, evac, wsrc=None, nk=None, rhs_fn=None):
            nk = nk or KC
            c = 0
            while c < ncols:
                n = min(256, ncols - c)
                src = (w_in[l, :, col0 + c:col0 + c + n]) if wsrc is None else wsrc(c, n)
                W = wload(src, nk, n)
                for t in range(n // 128):
                    for bi, (b0, bn, g) in enumerate(cfg.blocks):
                        p = PS()
                        for k in range(nk):
                            rhs = hT[:, k, b0:b0 + bn] if rhs_fn is None else rhs_fn(k, b0, bn)
                            mm(p[:, :bn], W[:, k, t * 128:(t + 1) * 128], rhs, start=(k == 0), stop=(k == nk - 1))
                        evac((c // 128) + t, bi, (b0, bn, g), p[:, :bn])
                        pump()
                c += n

        def proj_tm64(l, col0, ncols, evac):
            W = wload(w_in[l, :, col0:col0 + ncols], KC, ncols)
            per = max(1, min(4, 512 // ncols))
            for t0 in range(0, n64, per):
                nb = min(per, n64 - t0)
                p = PS()
                for i in range(nb):
                    t = t0 + i
                    for k in range(KC):
                        mm(p[0:64, i * ncols:(i + 1) * ncols], hT[:, k, t * 64:(t + 1) * 64], W[:, k, :ncols], start=(k == 0), stop=(k == KC - 1))
                evac(t0, nb, p[0:64, 0:nb * ncols].rearrange("p (t c) -> p t c", c=ncols))

        def evac_to(dst, func=None):
            def f(ti, bi, blk, p):
                b0, bn, g = blk
                if func is None:
                    cpy(dst[:, b0:b0 + bn], p, e="act")
                else:
                    act(dst[:, b0:b0 + bn], p, func)
            return f

        def evac_pad(dst):
            def f(ti, bi, blk, p):
                b0, bn, g = blk
                for si, (a, b) in enumerate(cfg.segs):
                    lo = max(a, b0); hi = min(b, b0 + bn)
                    if lo < hi:
                        po = cfg.padoff[si] + (lo - a)
                        cpy(dst[:, po:po + (hi - lo)], p[:, lo - b0:hi - b0], e="act")
            return f

        xid = [0]

        def allgather(send_sb, ncols):
            i = xid[0]; xid[0] += 1
            snd = nc.dram_tensor("xs%d" % i, [128, ncols], F32)
            rcv = nc.dram_tensor("xr%d" % i, [4 * 128, ncols], F32)
            dma(snd.ap(), send_sb)
            S.op("pool", lambda h: h.collective_compute("AllGather", ALU.bypass, replica_groups=cfg.groups,
                                                        ins=[snd.ap().opt()], outs=[rcv.ap().opt()]),
                 reads=[snd.ap()], writes=[rcv.ap()], coll=True)
            return rcv.ap()

        def mcol(kind, j):
            base = {'L': 0, 'R': 4, 'f': 8, 'b': 12}[kind]
            return sm('masks', base + j)

        def halo_exchange(l):
            ar = Arena()
            hbd = ar.bf16(KC * 4).rearrange("p (k c) -> p k c", c=4)
            for k in range(KC):
                cpy(hbd[:, k, 0:2], hT[:, k, S0:S0 + 2], e="dve")
                cpy(hbd[:, k, 2:4], hT[:, k, T - 2:T], e="dve")
            NCT = 4 * NH
            sendH = ar.f32(NCT * 4)
            cols = [off['bx'] + i * 128 for i in range(NH)] + [off[nm] + i * 128 for nm in ('cq', 'ck', 'cv') for i in range(NH)]
            for ci in range(0, NCT, 2):
                assert cols[ci + 1] == cols[ci] + 128
                W = wload(w_in[l, :, cols[ci]:cols[ci] + 256], KC, 256)
                for t in range(2):
                    p = PS()
                    for k in range(KC):
                        mm(p[:, 0:4], W[:, k, t * 128:(t + 1) * 128], hbd[:, k, :], start=(k == 0), stop=(k == KC - 1))
                    cpy(sendH[:, (ci + t) * 4:(ci + t) * 4 + 4], p[:, 0:4], e="act")
            rcv = allgather(sendH, NCT * 4)
            return rcv

        def halo_recv(rcv):
            ar = Arena()
            NCT = 4 * NH
            G = ar.f32(4 * NCT * 4)
            dma(G.rearrange("p (r c) -> p r c", r=4), rcv.rearrange("(r p) c -> p r c", p=128))
            Gv = G.rearrange("p (r t c) -> p r t c", r=4, c=4)
            hv = halo[:].rearrange("p (t c) -> p t c", c=3)
            memset(halo[:], 0.0, e="dve")
            for j in range(4):
                stt(hv[:, :, 0:2], Gv[:, j, :, 2:4], mcol('L', j), hv[:, :, 0:2], ALU.mult, ALU.add)
                stt(hv[:, :, 2:3], Gv[:, j, :, 0:1], mcol('R', j), hv[:, :, 2:3], ALU.mult, ALU.add)

        def put_halo(xp, ct):
            po = cfg.padoff[-1]
            cpy(xp[:, po - 2:po], halo[:, ct * 3:ct * 3 + 2], e="dve")
            cpy(xp[:, po + SL:po + SL + 1], halo[:, ct * 3 + 2:ct * 3 + 3], e="dve")

        def conv(dst, xp, wname, l, ct, nct, bias=None):
            for si, (a, b) in enumerate(cfg.segs):
                po = cfg.padoff[si]; n = b - a
                for j in range(4):
                    src = xp[:, po - 2 + j:po - 2 + j + n]
                    wcol = sm(wname % l, j * nct + ct)
                    if j == 0:
                        if bias is not None:
                            ts(dst[:, a:b], src, wcol, bias, ALU.mult, ALU.add)
                        else:
                            ts(dst[:, a:b], src, wcol, None, ALU.mult)
                    else:
                        stt(dst[:, a:b], src, wcol, dst[:, a:b], ALU.mult, ALU.add)

        def finalize(oT, zs, tmp, hd, c0, c1, gain):
            drain()
            c = c0
            while c < c1:
                n = min(512, c1 - c)
                act(tmp[:, c:c + n], oT[:, c:c + n], AF.Square)
                p = PS()
                mm(p[:, :n], ones128[:], tmp[:, c:c + n])
                act(tmp[:, c:c + n], p[:, :n], AF.Sqrt, bias=epsc[:])
                recip(tmp[:, c:c + n], tmp[:, c:c + n])
                tt(tmp[:, c:c + n], tmp[:, c:c + n], oT[:, c:c + n], ALU.mult)
                stt(outT[:, hd, c:c + n], tmp[:, c:c + n], gain, zs[:, c:c + n], ALU.mult, ALU.mult)
                c += n

        def merge_gen(l, kbr, first):
            for dmt in range(KC):
                i = wrr[0] % 2; wrr[0] += 1
                dma(wst[i][:, :NH, :128], w_branch[l, kbr, :, dmt * 128:(dmt + 1) * 128].rearrange("(k p) c -> p k c", p=128))
                cpy(Wb_m, wst[i][:, :NH, :128], e="pool")
                i = wrr[0] % 2; wrr[0] += 1
                gc0 = off['mg'] + kbr * DM + dmt * 128
                dma(wst[i][:, :KC, :128], w_in[l, :, gc0:gc0 + 128].rearrange("(k p) c -> p k c", p=128))
                cpy(Wg_m, wst[i][:, :KC, :128], e="pool")
                yield
                for (b0, bn, g) in cfg.blocks:
                    p1 = PS()
                    for c in range(NH):
                        mm(p1[:, :bn], Wb_m[:, c, :], outT[:, c, b0:b0 + bn], start=(c == 0), stop=(c == NH - 1))
                    p2 = PS()
                    for k in range(KC):
                        mm(p2[:, :bn], Wg_m[:, k, :], hT[:, k, b0:b0 + bn], start=(k == 0), stop=(k == KC - 1))
                    act(gs_m[:, :bn], p2[:, :bn], AF.Sigmoid)
                    if first:
                        tt(ysum[:, dmt, b0:b0 + bn], gs_m[:, :bn], p1[:, :bn], ALU.mult)
                    else:
                        tt(tm_m[:, :bn], gs_m[:, :bn], p1[:, :bn], ALU.mult)
                        tt(ysum[:, dmt, b0:b0 + bn], ysum[:, dmt, b0:b0 + bn], tm_m[:, :bn], ALU.add, e="pool")
                    yield

        sendbuf = sb("sendbuf", [128, 2 * NH * 256])
        gdn_gb = sb("gdn_gb", [64, n64 * 4 * NH]); gdn_gc = sb("gdn_gc", [64, n64 * 5 * 2 * NH]); gdn_gl = sb("gdn_gl", [128, n64 * 2 * NH])
        gdn_nA = sb("gdn_nA", [128, 2 * NH]); lgc_t = sb("lgc_t", [128, 2 * NH]); g1024_t = sb("g1024_t", [128, 2 * NH])

        def fold_load(rcv, kind, l, hd, dirn, col_B, col_M, col_F, s0_dram, ar):
            fb = {}
            fb['Sx'] = ar.f32(128); fb['t1'] = ar.f32(128)
            fb['B'] = ar.f32(4 * 128).rearrange("p (r c) -> p r c", r=4)
            dma(fb['Sx'], s0_dram[l, dirn, hd])
            dma(fb['B'], rcv[:, col_B:col_B + 128].rearrange("(r p) c -> p r c", p=128))
            if kind == 'C':
                fb['M'] = ar.f32(4 * 128).rearrange("p (r c) -> p r c", r=4)
                fb['MT'] = ar.f32(128)
                dma(fb['M'], rcv[:, col_M:col_M + 128].rearrange("(r p) c -> p r c", p=128))
            if kind == 'D':
                fb['F'] = ar.f32(4).rearrange("p (r c) -> p r c", r=4)
                dma(fb['F'], rcv[:, col_F:col_F + 1].rearrange("(r p) c -> p r c", p=128), slow=True)
            return fb

        def fold_compute(fb, kind, dirn, g1024=None):
            Sx = fb['Sx']; t1 = fb['t1']
            order = range(4) if dirn == 0 else range(3, -1, -1)
            for j in order:
                m = mcol('f' if dirn == 0 else 'b', j)
                Bj = fb['B'][:, j, :]
                if kind == 'A':
                    stt(t1, Sx, g1024, Bj, ALU.mult, ALU.add)
                elif kind == 'D':
                    stt(t1, Sx, fb['F'][:, j, :], Bj, ALU.mult, ALU.add)
                else:
                    p = PS()
                    tr(p[:, 0:128], fb['M'][:, j, :], ident)
                    cpy(fb['MT'], p[:, 0:128], e="act")
                    p2 = PS()
                    mm(p2[:, 0:128], fb['MT'], Sx)
                    tt(t1, p2[:, 0:128], Bj, ALU.add)
                tt(t1, t1, Sx, ALU.subtract)
                stt(Sx, t1, m, Sx, ALU.mult, ALU.add)
            return Sx

        def gla_mixer(l, kind):
            isA = (kind == 'A')
            qs = 1.0 if isA else 128.0 ** -0.5
            ks = 128.0 ** -0.5 if isA else 1.0
            ncol_send = 2 * NH * 128 + (0 if isA else 2 * NH)
            sendb = sendbuf[:, 0:ncol_send]
            o_state = o_ret if isA else o_hgrn
            s0_state = s0_ret if isA else s0_hgrn
            lgc = None
            if isA:
                lgc = lgc_t; g1024 = g1024_t
                act(lgc[:], sm('retd%d' % l, 0, 2 * NH), AF.Exp)
                ts(lgc[:], lgc[:], -1.0, 1.0, ALU.mult, ALU.add)
                act(lgc[:], lgc[:], AF.Ln)
                act(g1024[:], lgc[:], AF.Exp, scale=float(SL))
            for hd in range(NH):
                ar = Arena()
                qT = ar.f32(T); kT = ar.f32(T); lf = ar.f32(T); bc = ar.f32(T); tmp = ar.f32(T)
                oT = ar.f32(T); zs = ar.bf16(T)
                qt1 = ar.bf16(T); kt1 = ar.bf16(T); qt = [qt1, qt1]; kt = [kt1, kt1]
                kd1 = ar.bf16(n64 * 128).rearrange("p (t d) -> p t d", d=128); kd_tok = [kd1, kd1]
                v_tok = ar.bf16(n64 * 128).rearrange("p (t d) -> p t d", d=128)
                Fc1 = ar.f32(n32); Fc = [Fc1, Fc1]
                Qc1 = ar.bf16(SL); Qc = [Qc1, Qc1]
                S32p = [ar.f32(128), ar.f32(128)]
                Sall = ar.bf16(n32 * 128).rearrange("p (c d) -> p c d", d=128)
                PT8 = ar.bf16(8 * 64)
                proj_fm(l, off['aq' if isA else 'dq'] + hd * 128, 128, evac_to(qT, None if isA else AF.Silu))
                proj_fm(l, off['az' if isA else 'dz'] + hd * 128, 128, evac_to(zs, AF.Silu))
                proj_tm64(l, off['av' if isA else 'di'] + hd * 128, 128, lambda t0, nb, p: cpy(v_tok[0:64, t0:t0 + nb, :], p, e="act"))
                Cs = rope[:, 0:SL]; Sn = rope[:, SL:2 * SL]

                def do_rope(x):
                    t1 = bc[:, 0:SL]; xc = tmp[:, 0:SL]
                    tt(t1[0:64], x[64:128, S0:T], Sn[64:128], ALU.mult)
                    tt(t1[64:128], x[0:64, S0:T], Sn[0:64], ALU.mult, e="pool")
                    tt(xc, x[:, S0:T], Cs, ALU.mult)
                    tt(x[0:64, S0:T], xc[0:64], t1[0:64], ALU.subtract)
                    tt(x[64:128, S0:T], xc[64:128], t1[64:128], ALU.add, e="pool")
                if isA:
                    proj_fm(l, off['ak'] + hd * 128, 128, evac_to(kT))
                    do_rope(qT); do_rope(kT)
                memset(oT, 0.0)
                for dirn in range(2):
                    rev = (dirn == 1)
                    if isA:
                        ts(lf, qT, 0.0, lgc[:, dirn * NH + hd:dirn * NH + hd + 1], ALU.mult, ALU.add, e="pool")
                    else:
                        li = (l * 2 + dirn) * NH + hd
                        proj_fm(l, off['df'] + dirn * MW + hd * 128, 128, evac_to(lf, AF.Sigmoid))
                        ts(kT, lf, nomlv[:, li:li + 1], omlv[:, li:li + 1], ALU.mult, ALU.add)
                        ts(lf, lf, omlv[:, li:li + 1], lbv[:, li:li + 1], ALU.mult, ALU.add)
                        ts(lf, lf, 1e-30, None, ALU.max, e="pool")
                        act(lf, lf, AF.Ln)
                    RM = rmb[:, dirn * T:(dirn + 1) * T]
                    for c0 in range(0, T, 512):
                        c1 = min(T, c0 + 512)
                        scan(bc[:, c0:c1], RM[:, c0:c1], lf[:, c0:c1], 0.0, rev=rev)
                    act(tmp, bc, AF.Exp)
                    stt(qt[dirn], qT, qs, tmp, ALU.mult, ALU.mult)
                    act(tmp, bc, AF.Exp, scale=-1.0)
                    stt(kt[dirn], kT, ks, tmp, ALU.mult, ALU.mult)
                    bcv = bc.rearrange("p (c i) -> p c i", i=32)
                    tot = bcv[:, :, 31:32] if not rev else bcv[:, :, 0:1]
                    tmpv = tmp.rearrange("p (c i) -> p c i", i=32)
                    for c0 in range(0, n32, 16):
                        c1 = min(n32, c0 + 16)
                        tt(tmpv[:, c0:c1, :], tot[:, c0:c1, :].to_broadcast([128, c1 - c0, 32]), bcv[:, c0:c1, :], ALU.subtract)
                    act(tmp, tmp, AF.Exp)
                    stt(tmp, kT, ks, tmp, ALU.mult, ALU.mult)
                    act(Fc[dirn].rearrange("p (c o) -> p c o", o=1), tot, AF.Exp)
                    for t0 in range(0, n64, 4):
                        nb = min(4, n64 - t0)
                        p = PS()
                        for i in range(nb):
                            tr(p[0:64, i * 128:(i + 1) * 128], tmp[:, (t0 + i) * 64:(t0 + i + 1) * 64], ident)
                        cpy(kd_tok[dirn][0:64, t0:t0 + nb, :], p[0:64, 0:nb * 128].rearrange("p (t d) -> p t d", d=128), e="act")
                    bs = bc[:, 0:SL]
                    scan(bs, onesT[:], lf[:, S0:T], 0.0, rev=rev)
                    act(bs, bs, AF.Exp)
                    stt(Qc[dirn], qT[:, S0:T], qs, bs, ALU.mult, ALU.mult)
                    if not isA:
                        lastc = (SL - 1) if not rev else 0
                        cpy(sendb[:, 2 * NH * 128 + dirn * NH + hd:2 * NH * 128 + dirn * NH + hd + 1], bs[:, lastc:lastc + 1], e="dve")
                    dma(spill[hd, 1 + dirn, :, 0:SL // 2], Qc[dirn].bitcast(F32))
                    order = []
                    for si, (a, b) in enumerate(cfg.segs):
                        cs_ = list(range(a // 32, b // 32))
                        if rev: cs_ = cs_[::-1]
                        for i, c in enumerate(cs_):
                            order.append((si, c, i == 0, i == len(cs_) - 1, cs_[i + 1] if i + 1 < len(cs_) else None))
                    pkslots = {}
                    pkb = [None]

                    def emit_kv(idx):
                        si, c, first, last, nx = order[idx]
                        if idx % 8 == 0: pkb[0] = (PS(), PS())
                        t = c // 2; h2 = c % 2
                        j = (idx % 8) // 2
                        slot = pkb[0][idx % 2][:, j * 128:(j + 1) * 128]
                        mm(slot, kd_tok[dirn][32 * h2:32 * h2 + 32, t, :], v_tok[32 * h2:32 * h2 + 32, t, :])
                        pkslots[idx] = slot
                    nmm = 0; cur = 0
                    for idx in range(len(order)):
                        while nmm < min(len(order), (idx // 8) * 8 + 24):
                            emit_kv(nmm); nmm += 1
                        si, c, first, last, nx = order[idx]
                        if first:
                            memset(S32p[cur], 0.0, e="dve"); memset(Sall[:, c, :], 0.0, e="pool")
                        nxt = 1 - cur
                        stt(S32p[nxt], S32p[cur], Fc[dirn][:, c:c + 1], pkslots[idx], ALU.mult, ALU.add)
                        if not last:
                            cpy(Sall[:, nx, :], S32p[nxt], e="act")
                        elif si < NSEQ:
                            fin.append(dma(o_state[si, l, dirn, hd], S32p[nxt]))
                        else:
                            cpy(sendb[:, (dirn * NH + hd) * 128:(dirn * NH + hd + 1) * 128], S32p[nxt], e="pool")
                        cur = nxt
                    mk = cc('mgf' if not rev else 'mgb', 64)
                    for t0 in range(0, n64, GNT):
                        nt = min(GNT, n64 - t0)
                        p = PS()
                        for i in range(nt):
                            c64 = slice((t0 + i) * 64, (t0 + i + 1) * 64)
                            mm(p[0:64, i * 64:(i + 1) * 64], kt[dirn][:, c64], qt[dirn][:, c64])
                        ptv = PT8[0:64, 0:nt * 64].rearrange("p (t i) -> p t i", i=64)
                        mkb = bass.AP(mk.tensor, mk.offset, [list(mk.ap[0]), [0, nt], list(mk.ap[1])])
                        tt(ptv, p[0:64, 0:nt * 64].rearrange("p (t i) -> p t i", i=64), mkb, ALU.mult)
                        po = PS()
                        for i in range(nt):
                            t = t0 + i
                            for c in (0, 1):
                                c32 = slice(t * 64 + c * 32, t * 64 + (c + 1) * 32)
                                osl = po[:, i * 64 + c * 32:i * 64 + (c + 1) * 32]
                                mm(osl, v_tok[0:64, t, :], ptv[:, i, c * 32:(c + 1) * 32], start=True, stop=False)
                                mm(osl, Sall[:, 2 * t + c, :], qt[dirn][:, c32], start=False, stop=True)
                        tt(oT[:, t0 * 64:(t0 + nt) * 64], oT[:, t0 * 64:(t0 + nt) * 64], po[:, 0:nt * 64], ALU.add)
                dma(spill[hd, 0, :, 0:T], oT)
                dma(spill[hd, 1, :, SL:SL + T // 2], zs.bitcast(F32))
            rcv = allgather(sendb, ncol_send)
            drain()
            ar = Arena()
            fbs = {}
            for hd in range(NH):
                for dirn in range(2):
                    fbs[(hd, dirn)] = fold_load(rcv, kind, l, hd, dirn, (dirn * NH + hd) * 128, None, 2 * NH * 128 + dirn * NH + hd, s0_state, ar)
            base = ar.o
            for hd in range(NH):
                ar.o = base
                oTs = ar.f32(T); zs = ar.bf16(T); tmp = ar.f32(T)
                Qc = [ar.bf16(SL) for _ in range(2)]
                dma(oTs, spill[hd, 0, :, 0:T])
                for dirn in range(2):
                    dma(Qc[dirn].bitcast(F32), spill[hd, 1 + dirn, :, 0:SL // 2])
                dma(zs.bitcast(F32), spill[hd, 1, :, SL:SL + T // 2])
                Sin = []
                for dirn in range(2):
                    Sx = fold_compute(fbs[(hd, dirn)], kind, dirn, g1024=(g1024[:, dirn * NH + hd:dirn * NH + hd + 1] if isA else None))
                    sbf_ = ar.bf16(128)
                    cpy(sbf_, Sx, e="act")
                    Sin.append(sbf_)
                c = 0
                while c < SL:
                    n = min(512, SL - c)
                    p = PS()
                    mm(p[:, :n], Sin[0], Qc[0][:, c:c + n], start=True, stop=False)
                    mm(p[:, :n], Sin[1], Qc[1][:, c:c + n], start=False, stop=True)
                    tt(oTs[:, S0 + c:S0 + c + n], oTs[:, S0 + c:S0 + c + n], p[:, :n], ALU.add)
                    c += n
                gain = 1.0 if isA else sm('hng%d' % l, 0)
                finalize(oTs, zs, tmp, hd, 0, T, gain)

        def lru_mixer(l):
            ar = Arena()
            cpzf = ar.bf16(NH * SL).rearrange("p (c t) -> p c t", t=SL)
            cpzb = ar.bf16(NH * SL).rearrange("p (c t) -> p c t", t=SL)
            hz = ar.bf16(NH * T).rearrange("p (c t) -> p c t", t=T)
            sendb = ar.f32(4 * NH)
            ccol = ar.f32(2 * NH)
            act(ccol, sm('llam%d' % l, 0, 2 * NH), AF.Exp, scale=-1.0)
            act(ccol, ccol, AF.Ln, bias=1.0)
            ts(ccol, ccol, -8.0, None, ALU.mult)
            base = ar.o
            for ct in range(NH):
                ar.o = base
                gw = ar.f32(4 * 128)
                xp = ar.f32(cfg.TP); xb = ar.f32(T); ra = ar.f32(T); iu = ar.f32(T)
                hf = ar.f32(T); tmp = ar.f32(T); zs = ar.bf16(T)
                hb = xp[:, 0:T]
                for dg in range(4):
                    gi = dg * NH + ct
                    dma(gw[:, dg * 128:(dg + 1) * 128], lrugw_in[l, :, gi * 128:(gi + 1) * 128])
                memset(xp, 0.0)
                proj_fm(l, off['bx'] + ct * 128, 128, evac_pad(xp))
                proj_fm(l, off['bz'] + ct * 128, 128, evac_to(zs, AF.Silu))
                put_halo(xp, ct)
                conv(xb, xp, 'lcw%d', l, ct, NH, bias=sm('lcb%d' % l, ct))
                for dirn in range(2):
                    rev = (dirn == 1)
                    hh = hf if dirn == 0 else hb
                    for gate, dst in ((0, ra), (1, iu)):
                        gi = (dirn * 2 + gate) * NH + ct
                        for (b0, bn, g) in cfg.blocks:
                            p = PS()
                            mm(p[:, :bn], gw[:, (dirn * 2 + gate) * 128:(dirn * 2 + gate + 1) * 128], xb[:, b0:b0 + bn])
                            act(dst[:, b0:b0 + bn], p[:, :bn], AF.Sigmoid, bias=sm('lgb%d' % l, gi))
                    act(ra, ra, AF.Exp, scale=ccol[:, dirn * NH + ct:dirn * NH + ct + 1])
                    tt(tmp, ra, ra, ALU.mult)
                    ts(tmp, tmp, -1.0, 1.0, ALU.mult, ALU.add)
                    ts(tmp, tmp, 1e-12, None, ALU.max)
                    act(tmp, tmp, AF.Sqrt)
                    tt(iu, iu, xb, ALU.mult, e="pool")
                    tt(iu, iu, tmp, ALU.mult)
                    for si, (a, b) in enumerate(cfg.segs):
                        scan(hh[:, a:b], ra[:, a:b], iu[:, a:b], 0.0, rev=rev)
                        lastc = (b - 1) if not rev else a
                        if si < NSEQ:
                            oc = ((si * depth + l) * 2 + dirn) * NH + ct
                            fin.append(dma(o_lru[:, oc:oc + 1], hh[:, lastc:lastc + 1], slow=True))
                        else:
                            cpy(sendb[:, (dirn * NH + ct) * 2 + 1:(dirn * NH + ct) * 2 + 2], hh[:, lastc:lastc + 1], e="dve")
                    cpd = tmp[:, 0:SL]
                    scan(cpd, ra[:, S0:T], zerosT[:], 1.0, rev=rev)
                    lastc = (SL - 1) if not rev else 0
                    cpy(sendb[:, (dirn * NH + ct) * 2:(dirn * NH + ct) * 2 + 1], cpd[:, lastc:lastc + 1], e="dve")
                    tt((cpzf if dirn == 0 else cpzb)[:, ct, :], cpd, zs[:, S0:T], ALU.mult)
                tt(hf, hf, hb, ALU.add)
                tt(hz[:, ct, :], hf, zs, ALU.mult)
            rcv = allgather(sendb, 4 * NH)
            drain()
            G = ar.f32(4 * 4 * NH)
            dma(G.rearrange("p (r c) -> p r c", r=4), rcv.rearrange("(r p) c -> p r c", p=128))
            Gv = G.rearrange("p (r c two) -> p r c two", r=4, two=2)
            hin = ar.f32(2 * NH); t1 = ar.f32(NH)
            for dirn in range(2):
                hd_ = hin[:, dirn * NH:(dirn + 1) * NH]
                o, _ = cfg.sl['slru']
                cpy(hd_, smalls[:, o + (l * 2 + dirn) * NH:o + (l * 2 + dirn + 1) * NH], e="dve")
                for j in (range(4) if dirn == 0 else range(3, -1, -1)):
                    Aj = Gv[:, j, dirn * NH:(dirn + 1) * NH, 0]; bj = Gv[:, j, dirn * NH:(dirn + 1) * NH, 1]
                    tt(t1, Aj, hd_, ALU.mult)
                    tt(t1, t1, bj, ALU.add)
                    tt(t1, t1, hd_, ALU.subtract)
                    stt(hd_, t1, mcol('f' if dirn == 0 else 'b', j), hd_, ALU.mult, ALU.add)
            for ct in range(NH):
                cpy(outT[:, ct, 0:S0], hz[:, ct, 0:S0], e="pool")
                stt(outT[:, ct, S0:T], cpzf[:, ct, :], hin[:, ct:ct + 1], hz[:, ct, S0:T], ALU.mult, ALU.add)
                stt(outT[:, ct, S0:T], cpzb[:, ct, :], hin[:, NH + ct:NH + ct + 1], outT[:, ct, S0:T], ALU.mult, ALU.add)

        def gdn_mixer(l):
            nch = n64
            W8 = 2 * NH
            gsb = gdn_gb
            gv = gsb[:].rearrange("p (c w) -> p c w", w=2 * W8)
            gcs = gdn_gc
            gcv = gcs[:].rearrange("p (c f w) -> p c f w", f=5, w=W8)
            glast = gdn_gl
            glv = glast[:].rearrange("p (c w) -> p c w", w=W8)
            sendb = sendbuf[:, 0:2 * NH * 256]
            nA = gdn_nA
            act(nA[:], sm('galog%d' % l, 0, W8), AF.Exp)
            ts(nA[:], nA[:], -1.0, None, ALU.mult)
            proj_tm64(l, off['ca'], 2 * W8, lambda t0, nb, p: cpy(gv[:, t0:t0 + nb, :], p, e="act"))
            for t in range(nch):
                tt(gv[:, t, 0:W8], gv[:, t, 0:W8], sm('gdt%d' % l, 0, W8)[0:64], ALU.add)
            act(gv[:, :, 0:W8], gv[:, :, 0:W8], AF.Exp)
            act(gv[:, :, 0:W8], gv[:, :, 0:W8], AF.Ln, bias=1.0)
            for t in range(nch):
                tt(gv[:, t, 0:W8], gv[:, t, 0:W8], nA[0:64, :], ALU.mult)
            act(gv[:, :, W8:2 * W8], gv[:, :, W8:2 * W8], AF.Sigmoid)
            for dirn in range(2):
                tri = cc('trif' if dirn == 0 else 'trib', 64)
                for t0 in range(0, nch, 16):
                    t1_ = min(nch, t0 + 16); nn = (t1_ - t0)
                    p = PS()
                    for t in range(t0, t1_):
                        mm(p[0:64, (t - t0) * NH:(t - t0 + 1) * NH], tri, gv[:, t, dirn * NH:(dirn + 1) * NH])
                    cpy(gcv[:, t0:t1_, 0, dirn * NH:(dirn + 1) * NH], p[0:64, 0:nn * NH].rearrange("p (c w) -> p c w", w=NH), e="act")
                    p2 = PS()
                    for t in range(t0, t1_):
                        mm(p2[:, (t - t0) * NH:(t - t0 + 1) * NH], ones1[0:64, :], gv[:, t, dirn * NH:(dirn + 1) * NH])
                    cpy(gcv[:, t0:t1_, 1, dirn * NH:(dirn + 1) * NH], p2[0:64, 0:nn * NH].rearrange("p (c w) -> p c w", w=NH), e="act")
                    act(glv[:, t0:t1_, dirn * NH:(dirn + 1) * NH], p2[:, 0:nn * NH].rearrange("p (c w) -> p c w", w=NH), AF.Exp)
            act(gcv[:, :, 2, :], gcv[:, :, 0, :], AF.Exp)
            tt(gcv[:, :, 2, :], gcv[:, :, 2, :], gv[:, :, W8:2 * W8], ALU.mult)
            tt(gcv[:, :, 3, :], gcv[:, :, 1, :], gcv[:, :, 0, :], ALU.subtract)
            act(gcv[:, :, 3, :], gcv[:, :, 3, :], AF.Exp)
            i64 = cc('ident', 64, 64)

            def bmid(m, n):
                return bass.AP(m.tensor, m.offset, [list(m.ap[0]), [0, n], list(m.ap[1])])
            GB = 4
            nch_p = S0 // 64; nch_s = SL // 64
            for hd in range(NH):
                ar = Arena()
                d0 = ar.o
                xp = ar.f32(cfg.TP); qT = ar.f32(T); kT = ar.f32(T); tmp = ar.f32(T)
                d1 = ar.o
                oT = ar.f32(T); zs = ar.bf16(T)
                qTb = ar.bf16(T); kTb = ar.bf16(T)
                k_tok = ar.bf16(nch * 128).rearrange("p (t d) -> p t d", d=128)
                v_tok = ar.bf16(nch * 128).rearrange("p (t d) -> p t d", d=128)
                Pc = ar.bf16(SL)
                ncm = max(nch_p, nch_s)
                Qp = ar.bf16(ncm * 64).rearrange("p (t i) -> p t i", i=64)
                Mt = ar.bf16(ncm * 128).rearrange("p (t d) -> p t d", d=128)
                Nc = ar.bf16(ncm * 128).rearrange("p (t d) -> p t d", d=128)
                Sall = ar.bf16(max(nch_p * 128, nch_s * 256))
                S32s = [ar.f32(256) for _ in range(NSEQ + 1)]
                proj_fm(l, off['cz'] + hd * 128, 128, evac_to(zs, AF.Silu))
                for xi, (nm, dst) in enumerate((('cq', qT), ('ck', kT), ('cv', tmp))):
                    memset(xp, 0.0)
                    proj_fm(l, off[nm] + hd * 128, 128, evac_pad(xp))
                    ctc = NH + xi * NH + hd
                    put_halo(xp, ctc)
                    conv(dst, xp, 'gcw%d', l, xi * NH + hd, 3 * NH)
                    act(dst, dst, AF.Silu)
                    if xi < 2:
                        sq = oT
                        c = 0
                        while c < T:
                            n = min(512, T - c)
                            act(sq[:, c:c + n], dst[:, c:c + n], AF.Square)
                            p = PS()
                            mm(p[:, :n], ones1[:], sq[:, c:c + n])
                            act(sq[:, c:c + n], p[:, :n], AF.Sqrt, bias=epsc[:])
                            recip(sq[:, c:c + n], sq[:, c:c + n])
                            if xi == 0:
                                stt(qTb[:, c:c + n], dst[:, c:c + n], 128.0 ** -0.5, sq[:, c:c + n], ALU.mult, ALU.mult)
                            else:
                                tt(dst[:, c:c + n], dst[:, c:c + n], sq[:, c:c + n], ALU.mult)
                            c += n
                        if xi == 1:
                            cpy(kTb, dst, e="pool")
                for (srcT, dstK) in ((kT, k_tok), (tmp, v_tok)):
                    for t0 in range(0, nch, 4):
                        nb = min(4, nch - t0)
                        p = PS()
                        for i in range(nb):
                            tr(p[0:64, i * 128:(i + 1) * 128], srcT[:, (t0 + i) * 64:(t0 + i + 1) * 64], ident)
                        cpy(dstK[0:64, t0:t0 + nb, :], p[0:64, 0:nb * 128].rearrange("p (t d) -> p t d", d=128), e="act")
                memset(oT, 0.0)
                ctxs = []
                car = Arena(); car.o = d0
                GB = 4 if 2 * 736 * 4 <= d1 - d0 else 2
                if 2 * 736 * GB > d1 - d0:
                    car = ar
                for _ in range(2):
                    cx = {}
                    cx['F'] = [car.f32(GB * 64) for _ in range(6)]
                    cx['EG'] = car.bf16(GB * 64); cx['attn'] = car.bf16(GB * 64); cx['Rb'] = car.bf16(GB * 64)
                    cx['rv'] = car.bf16(GB * 128); cx['rk'] = car.bf16(GB * 128); cx['u'] = car.bf16(GB * 128); cx['w'] = car.bf16(GB * 128)
                    ctxs.append(cx)
                assert car is ar or car.o <= d1, (car.o, d1)

                def v3(a2, nb, w):
                    return a2[:, 0:nb * w].rearrange("p (c i) -> p c i", i=w)

                def group_prep(cx, t0, nb, tb, hi, rev):
                    tri = cc('trif' if not rev else 'trib', 64)
                    negt = cc('negtf' if not rev else 'negtb', 64)
                    posl = cc('poslf' if not rev else 'poslb', 64)
                    F = cx['F']
                    g3 = gv[:, t0:t0 + nb, hi:hi + 1]; b3 = gv[:, t0:t0 + nb, W8 + hi:W8 + hi + 1]
                    gc3 = gcv[:, t0:t0 + nb, 0, hi:hi + 1]; rk3 = gcv[:, t0:t0 + nb, 2, hi:hi + 1]; kd3 = gcv[:, t0:t0 + nb, 3, hi:hi + 1]
                    W = nb * 64
                    cols = slice(t0 * 64, (t0 + nb) * 64)
                    Y = v3(F[0][0:64], nb, 64)
                    tt(Y, bmid(tri, nb), g3.to_broadcast([64, nb, 64]), ALU.mult)
                    pG = PS()
                    for i in range(nb):
                        mm(pG[:, i * 64:(i + 1) * 64], ones1[0:64, :], Y[:, i, :])
                    Gs = F[1]
                    cpy(Gs[:, 0:W], pG[:, 0:W], e="act")
                    yield
                    Gs3 = v3(Gs[0:64], nb, 64)
                    gT = v3(F[2][0:64], nb, 64); gL = v3(F[3][0:64], nb, 64)
                    tt(gT, Gs3, gc3.to_broadcast([64, nb, 64]), ALU.subtract)
                    tt(gT, gT, bmid(negt, nb), ALU.add)
                    act(gT, gT, AF.Exp)
                    tt(gL, Gs3, gc3.to_broadcast([64, nb, 64]), ALU.subtract)
                    tt(gL, gL, bmid(posl, nb), ALU.add)
                    act(gL, gL, AF.Exp, scale=-1.0)
                    EG = cx['EG'][:, 0:W]
                    act(EG, Gs[:, 0:W], AF.Exp)
                    tt(EG, EG, qTb[:, cols], ALU.mult)
                    yield
                    pk = PS()
                    for i in range(nb):
                        c64 = slice((t0 + i) * 64, (t0 + i + 1) * 64)
                        mm(pk[0:64, i * 64:(i + 1) * 64], kTb[:, c64], kTb[:, c64])
                    Lm = v3(F[0][0:64], nb, 64)
                    tt(Lm, v3(pk[0:64], nb, 64), b3.to_broadcast([64, nb, 64]), ALU.mult)
                    tt(Lm, Lm, gL, ALU.mult)
                    yield
                    pu = PS()
                    for i in range(nb):
                        tr(pu[0:64, i * 64:(i + 1) * 64], Lm[:, i, :], i64)
                    Um = v3(F[1][0:64], nb, 64)
                    cpy(Um, v3(pu[0:64], nb, 64), e="act")
                    R = v3(F[3][0:64], nb, 64)
                    tt(R, bmid(i64, nb), Um, ALU.subtract)
                    yield
                    Lc, Uc = Lm, Um
                    Ln_, Un_ = v3(F[4][0:64], nb, 64), v3(F[5][0:64], nb, 64)
                    for lev in range(5):
                        pl = PS()
                        for i in range(nb):
                            mm(pl[0:64, i * 64:(i + 1) * 64], Uc[:, i, :], Lc[:, i, :])
                        cpy(Ln_, v3(pl[0:64], nb, 64), e="act")
                        if lev < 4:
                            pu2 = PS()
                            for i in range(nb):
                                mm(pu2[0:64, i * 64:(i + 1) * 64], Lc[:, i, :], Uc[:, i, :])
                            cpy(Un_, v3(pu2[0:64], nb, 64), e="dve")
                        pr = PS()
                        for i in range(nb):
                            mm(pr[0:64, i * 64:(i + 1) * 64], Ln_[:, i, :], R[:, i, :])
                        tt(R, R, v3(pr[0:64], nb, 64), ALU.add)
                        Lc, Uc, Ln_, Un_ = Ln_, Un_, Lc, Uc
                        yield
                    Rb = v3(cx['Rb'][0:64], nb, 64)
                    cpy(Rb, R, e="act")
                    rv = v3(cx['rv'][0:64], nb, 128); rk = v3(cx['rk'][0:64], nb, 128)
                    u = v3(cx['u'][0:64], nb, 128); w = v3(cx['w'][0:64], nb, 128)
                    tt(rv, v_tok[0:64, t0:t0 + nb, :], b3.to_broadcast([64, nb, 128]), ALU.mult, e="pool")
                    tt(rk, k_tok[0:64, t0:t0 + nb, :], rk3.to_broadcast([64, nb, 128]), ALU.mult, e="pool")
                    pU = PS()
                    for i in range(nb):
                        mm(pU[0:64, i * 128:(i + 1) * 128], Rb[:, i, :], rv[:, i, :])
                    cpy(u, v3(pU[0:64], nb, 128), e="act")
                    pW = PS()
                    for i in range(nb):
                        mm(pW[0:64, i * 128:(i + 1) * 128], Rb[:, i, :], rk[:, i, :])
                    cpy(w, v3(pW[0:64], nb, 128), e="dve")
                    kd = rv
                    tt(kd, k_tok[0:64, t0:t0 + nb, :], kd3.to_broadcast([64, nb, 128]), ALU.mult, e="pool")
                    yield
                    psc = PS()
                    for i in range(nb):
                        c64 = slice((t0 + i) * 64, (t0 + i + 1) * 64)
                        mm(psc[0:64, i * 64:(i + 1) * 64], kTb[:, c64], qTb[:, c64])
                    attn = v3(cx['attn'][0:64], nb, 64)
                    tt(attn, v3(psc[0:64], nb, 64), gT, ALU.mult)
                    yield
                    pq = PS()
                    for i in range(nb):
                        mm(pq[:, i * 64:(i + 1) * 64], w[:, i, :], attn[:, i, :])
                    tt(Qp[:, tb:tb + nb, :], v3(EG, nb, 64), v3(pq[:, 0:W], nb, 64), ALU.subtract)
                    po = PS()
                    for i in range(nb):
                        mm(po[:, i * 64:(i + 1) * 64], u[:, i, :], attn[:, i, :])
                    tt(oT[:, cols], oT[:, cols], po[:, 0:W], ALU.add)
                    yield
                    pm = PS()
                    for i in range(nb):
                        mm(pm[:, i * 128:(i + 1) * 128], w[:, i, :], kd[:, i, :])
                    act(Mt[:, tb:tb + nb, :], v3(pm[:, 0:nb * 128], nb, 128), AF.Copy, scale=-1.0)
                    pn = PS()
                    for i in range(nb):
                        mm(pn[:, i * 128:(i + 1) * 128], kd[:, i, :], u[:, i, :])
                    cpy(Nc[:, tb:tb + nb, :], v3(pn[:, 0:nb * 128], nb, 128), e="dve")
                    yield

                def run_interleaved(gens):
                    alive = list(gens)
                    while alive:
                        pump()
                        for gn in list(alive):
                            try:
                                next(gn)
                            except StopIteration:
                                alive.remove(gn)

                for dirn in range(2):
                    rev = (dirn == 1)
                    hi = dirn * NH + hd
                    for part in range(2):
                        issamp = (part == 1)
                        cbase = 0 if not issamp else nch_p
                        ncp = nch_p if not issamp else nch_s
                        N = 256 if issamp else 128
                        Sv = Sall[:, 0:ncp * N].rearrange("p (c n) -> p c n", n=N)
                        groups = [(cbase + g0, min(GB, ncp - g0), g0) for g0 in range(0, ncp, GB)]
                        for gi in range(0, len(groups), 2):
                            gens = [group_prep(ctxs[j], groups[gi + j][0], groups[gi + j][1], groups[gi + j][2], hi, rev)
                                    for j in range(min(2, len(groups) - gi))]
                            run_interleaved(gens)
                        segl = []
                        for si, (a, b) in enumerate(cfg.segs):
                            if (si == len(cfg.segs) - 1) != issamp: continue
                            cs_ = list(range(a // 64, b // 64))
                            if rev: cs_ = cs_[::-1]
                            segl.append((si, cs_))

                        def chain(si, cs_):
                            S32 = S32s[si][:, 0:N]
                            memset(S32, 0.0, e="dve")
                            if issamp:
                                cpy(S32[:, 128:256], ident, e="dve")
                            cpy(Sv[:, cs_[0] - cbase, :], S32, e="act")
                            yield
                            for k_, t in enumerate(cs_):
                                tb = t - cbase
                                gcol = glv[:, t, hi:hi + 1]
                                ps_ = PS()
                                mm(ps_[:, 0:128], identb[:], Nc[:, tb, :], start=True, stop=False)
                                mm(ps_[:, 0:128], Mt[:, tb, :], Sv[:, tb, 0:128], start=False, stop=True)
                                if issamp:
                                    mm(ps_[:, 128:256], Mt[:, tb, :], Sv[:, tb, 128:256], start=True, stop=True)
                                if k_ + 1 < len(cs_):
                                    stt(Sv[:, cs_[k_ + 1] - cbase, :], S32, gcol, ps_[:, 0:N], ALU.mult, ALU.add)
                                stt(S32, S32, gcol, ps_[:, 0:N], ALU.mult, ALU.add)
                                yield
                            if si < NSEQ:
                                fin.append(dma(o_gdn[si, l, dirn, hd], S32[:, 0:128]))
                            else:
                                cpy(sendb[:, (dirn * NH + hd) * 256:(dirn * NH + hd) * 256 + 256], S32, e="pool")
                        run_interleaved([chain(si, cs_) for (si, cs_) in segl])
                        for g0 in range(0, ncp, 8):
                            nb = min(8, ncp - g0)
                            po = PS()
                            for i in range(nb):
                                mm(po[:, i * 64:(i + 1) * 64], Sv[:, g0 + i, 0:128], Qp[:, g0 + i, :])
                            cols = slice((cbase + g0) * 64, (cbase + g0 + nb) * 64)
                            tt(oT[:, cols], oT[:, cols], po[:, 0:nb * 64], ALU.add)
                            if issamp:
                                pp = PS()
                                for i in range(nb):
                                    mm(pp[:, i * 64:(i + 1) * 64], Sv[:, g0 + i, 128:256], Qp[:, g0 + i, :])
                                cpy(Pc[:, g0 * 64:(g0 + nb) * 64], pp[:, 0:nb * 64], e="act")
                    dma(spill[hd, 1 + dirn, :, 0:SL // 2], Pc.bitcast(F32))
                dma(spill[hd, 0, :, 0:T], oT)
                dma(spill[hd, 1, :, SL:SL + T // 2], zs.bitcast(F32))
            rcv = allgather(sendb, 2 * NH * 256)
            drain()
            ar = Arena()
            fbs = {}
            for hd in range(NH):
                for dirn in range(2):
                    cb = (dirn * NH + hd) * 256
                    fbs[(hd, dirn)] = fold_load(rcv, 'C', l, hd, dirn, cb, cb + 128, None, s0_gdn, ar)
            base = ar.o
            for hd in range(NH):
                ar.o = base
                oTs = ar.f32(T); zs = ar.bf16(T); tmp = ar.f32(T)
                Qc = [ar.bf16(SL) for _ in range(2)]
                dma(oTs, spill[hd, 0, :, 0:T])
                for dirn in range(2):
                    dma(Qc[dirn].bitcast(F32), spill[hd, 1 + dirn, :, 0:SL // 2])
                dma(zs.bitcast(F32), spill[hd, 1, :, SL:SL + T // 2])
                Sin = []
                for dirn in range(2):
                    Sx = fold_compute(fbs[(hd, dirn)], 'C', dirn)
                    sbf_ = ar.bf16(128)
                    cpy(sbf_, Sx, e="act")
                    Sin.append(sbf_)
                c = 0
                while c < SL:
                    n = min(512, SL - c)
                    p = PS()
                    mm(p[:, :n], Sin[0], Qc[0][:, c:c + n], start=True, stop=False)
                    mm(p[:, :n], Sin[1], Qc[1][:, c:c + n], start=False, stop=True)
                    tt(oTs[:, S0 + c:S0 + c + n], oTs[:, S0 + c:S0 + c + n], p[:, :n], ALU.add)
                    c += n
                finalize(oTs, zs, tmp, hd, 0, T, sm('gng%d' % l, 0))

        def epilogue(l, xsrc, xdst):
            ar = Arena()
            wo = ar.bf16(KC * DM).rearrange("p (k c) -> p k c", c=DM)
            xb = ar.f32(KC * 512).rearrange("p (k c) -> p k c", c=512)
            sq = ar.f32(KC * 512).rearrange("p (k c) -> p k c", c=512)
            for c in range(0, DM, 256):
                i = wrr[0] % 2; wrr[0] += 1
                dma(wst[i][:, :KC, :256], w_out[l, :, c:c + 256].rearrange("(k p) c -> p k c", p=128))
                cpy(wo[:, :, c:c + 256], wst[i][:, :KC, :256], e="pool")
            for (b0, bn, g) in cfg.blocks:
                dma(xb[:, :, :bn], xsrc.rearrange("(k p) t -> p k t", p=128)[:, :, b0:b0 + bn])
                for k in range(KC):
                    p = PS()
                    for c in range(KC):
                        mm(p[:, :bn], wo[:, c, k * 128:(k + 1) * 128], ysum[:, c, b0:b0 + bn], start=(c == 0), stop=(c == KC - 1))
                    stt(xb[:, k, :bn], p[:, :bn], gatec(l, k, g), xb[:, k, :bn], ALU.mult, ALU.add)
                if l + 1 < depth:
                    dma(xdst.rearrange("(k p) t -> p k t", p=128)[:, :, b0:b0 + bn], xb[:, :, :bn])
                norm_block(xb, sq, b0, bn, g, l + 1)

        mod_layer(0)
        ar = Arena()
        xb0 = ar.f32(KC * 512).rearrange("p (k c) -> p k c", c=512)
        sq0 = ar.f32(KC * 512).rearrange("p (k c) -> p k c", c=512)
        for (b0, bn, g) in cfg.blocks:
            dma(xb0[:, :, :bn], xT_in.rearrange("(k p) t -> p k t", p=128)[:, :, b0:b0 + bn])
            norm_block(xb0, sq0, b0, bn, g, 0)
        for l in range(depth):
            S.phase = 'halo'; halo_rcv = halo_exchange(l)
            if l + 1 < depth:
                bg.append(mod_layer_gen(l + 1))
            S.phase = 'A'; gla_mixer(l, 'A'); bg.append(merge_gen(l, 0, True))
            S.phase = 'D'; gla_mixer(l, 'D'); drain(); bg.append(merge_gen(l, 3, False))
            S.phase = 'B'; halo_recv(halo_rcv); lru_mixer(l); drain(); bg.append(merge_gen(l, 1, False))
            S.phase = 'C'; gdn_mixer(l); drain(); bg.append(merge_gen(l, 2, False)); drain()
            xsrc = xT_in if l == 0 else xscr[(l - 1) % 2]
            S.phase = 'epi'; epilogue(l, xsrc, xscr[l % 2])

        nc._sched = S
        with nc.Block() as block:
            S.emit(block, st, final_wait_ops=fin)
    return nc


def _fm(v):
    v = np.asarray(v, np.float32).reshape(-1, 128)
    return np.ascontiguousarray(v.T)


def _bc(v):
    v = np.asarray(v, np.float32).reshape(1, -1)
    return np.ascontiguousarray(np.broadcast_to(v, (128, v.shape[1])))


def make_consts(cfg):
    C = np.zeros((128, cfg.ncst), np.float32)
    def put(n, m):
        o, w = cfg.cl[n]; C[:m.shape[0], o:o + m.shape[1]] = m
    put('ident', np.eye(128, dtype=np.float32))
    j = np.arange(64)[:, None]; i = np.arange(64)[None, :]
    same = (j // 32) == (i // 32)
    put('mgf', (same & (j <= i)).astype(np.float32))
    put('mgb', (same & (j >= i)).astype(np.float32))
    put('trif', (j <= i).astype(np.float32))
    put('trib', (j >= i).astype(np.float32))
    BIG = 30000.0
    put('negtf', np.where(j <= i, 0.0, -BIG).astype(np.float32))
    put('negtb', np.where(j >= i, 0.0, -BIG).astype(np.float32))
    put('poslf', np.where(i < j, 0.0, BIG).astype(np.float32))
    put('poslb', np.where(i > j, 0.0, BIG).astype(np.float32))
    return C


def make_rm(cfg):
    t = np.arange(cfg.T)
    rf = (t % 32 != 0).astype(np.float32)
    rb = (t % 32 != 31).astype(np.float32)
    return np.ascontiguousarray(np.broadcast_to(np.concatenate([rf, rb])[None, :], (128, 2 * cfg.T))).astype(np.float32)


def make_rope(cfg, q):
    GRID_W = 64
    n_freq = 32
    inv = (10000.0 ** (-np.arange(n_freq, dtype=np.float32) / n_freq)).astype(np.float32)
    tg = q * cfg.s_len + np.arange(cfg.s_len)
    r = (tg // GRID_W).astype(np.float32); c = (tg % GRID_W).astype(np.float32)
    ang = np.concatenate([r[:, None] * inv, c[:, None] * inv], axis=-1).astype(np.float32)
    cos = np.cos(ang).astype(np.float32).T; sin = np.sin(ang).astype(np.float32).T
    Cs = np.concatenate([cos, cos], 0); Sn = np.concatenate([sin, sin], 0)
    return np.ascontiguousarray(np.concatenate([Cs, Sn], 1)).astype(np.float32)


def make_in_maps(cfg, I):
    depth, NH, KC = cfg.depth, cfg.nh, cfg.kc
    f = lambda a: np.asarray(a, np.float32)
    shared = {
        "w_in": np.ascontiguousarray(f(I['w_in'])), "w_mod": np.ascontiguousarray(f(I['w_mod'])),
        "w_branch": np.ascontiguousarray(f(I['w_branch'])), "w_out": np.ascontiguousarray(f(I['w_out'])),
        "cst": make_consts(cfg), "rm": make_rm(cfg),
    }
    gw = f(I['lru_gate_w'])
    lrugw = np.zeros((depth, 128, 4 * NH, 128), np.float32)
    for l in range(depth):
        for d in range(2):
            for g in range(2):
                for ct in range(NH):
                    gi = (d * 2 + g) * NH + ct
                    for kk in range(2):
                        lrugw[l, kk * 64:(kk + 1) * 64, gi, kk * 64:(kk + 1) * 64] = gw[l, d, g, ct * 2 + kk]
    shared["lrugw"] = np.ascontiguousarray(lrugw.reshape(depth, 128, 4 * NH * 128))
    maps = []
    for r in range(cfg.n_cores):
        b = r // 4; q = r % 4
        xs = [f(I['x_prompt'][cfg.nps * r + i]).T for i in range(cfg.nps)]
        xs.append(f(I['x_sample'][b, q * cfg.s_len:(q + 1) * cfg.s_len]).T)
        m = dict(shared)
        m["xT"] = np.ascontiguousarray(np.concatenate(xs, axis=1))
        cT = np.stack([_fm(I['c_ctx']), _fm(I['c'][b])], axis=-1)
        m["cT"] = np.ascontiguousarray(cT.reshape(128, KC * 2))
        sm = np.zeros((128, cfg.ns), np.float32)
        def put(n, a):
            o, w = cfg.sl[n]; a = np.asarray(a, np.float32).reshape(128, -1); assert a.shape[1] == w, (n, a.shape, w); sm[:, o:o + w] = a
        for l in range(depth):
            put('normg%d' % l, _fm(I['norm_g'][l])); put('bmod%d' % l, _fm(I['b_mod'][l]))
            put('retd%d' % l, _bc(f(I['ret_decay'][l]).reshape(-1)))
            lcw = f(I['lru_conv_w'][l])
            put('lcw%d' % l, np.stack([_fm(lcw[j]) for j in range(4)], axis=1))
            put('lcb%d' % l, _fm(I['lru_conv_b'][l]))
            lgb = f(I['lru_gate_b'][l])
            put('lgb%d' % l, np.stack([_fm(lgb[d, g]) for d in range(2) for g in range(2)], axis=1))
            lam = f(I['lru_lambda'][l])
            put('llam%d' % l, np.stack([_fm(lam[d]) for d in range(2)], axis=1))
            gcw = f(I['gdn_conv_w'][l])
            put('gcw%d' % l, np.stack([_fm(gcw[j]) for j in range(4)], axis=1))
            put('galog%d' % l, _bc(f(I['gdn_a_log'][l]).reshape(-1))); put('gdt%d' % l, _bc(f(I['gdn_dt_bias'][l]).reshape(-1)))
            put('gng%d' % l, _fm(I['gdn_norm_g'][l])); put('hng%d' % l, _fm(I['hgrn_norm_g'][l]))
        hlb = f(I['hgrn_lb'])
        put('hlb', np.stack([_fm(hlb[l, d]) for l in range(depth) for d in range(2)], axis=1))
        put('fng', _fm(I['final_norm_g']))
        mk = np.zeros(16, np.float32)
        for j in range(4):
            mk[j] = 1.0 if j == q - 1 else 0.0
            mk[4 + j] = 1.0 if j == q + 1 else 0.0
            mk[8 + j] = 1.0 if j < q else 0.0
            mk[12 + j] = 1.0 if j > q else 0.0
        put('masks', _bc(mk))
        sl = f(I['state_lru'][b])
        put('slru', np.stack([_fm(sl[l, d]) for l in range(depth) for d in range(2)], axis=1))
        m["smalls"] = sm
        m["rope"] = make_rope(cfg, q)
        m["s0_ret"] = np.ascontiguousarray(f(I['state_ret'][b])); m["s0_gdn"] = np.ascontiguousarray(f(I['state_gdn'][b]))
        m["s0_hgrn"] = np.ascontiguousarray(f(I['state_hgrn'][b]))
        maps.append(m)
    return maps


def assemble(cfg, results):
    depth, NH = cfg.depth, cfg.nh
    yp = np.zeros((cfg.n_pbatch, cfg.p_len, cfg.dm), np.float32)
    ys = np.zeros((cfg.n_sbatch, cfg.s_total, cfg.dm), np.float32)
    nr = np.zeros((cfg.n_pbatch, depth, 2, NH, 128, 128), np.float32)
    ng = np.zeros_like(nr); nhg = np.zeros_like(nr)
    nl = np.zeros((cfg.n_pbatch, depth, 2, NH * 128), np.float32)
    for r in range(cfg.n_cores):
        R = results[r]
        b = r // 4; q = r % 4
        yT = np.asarray(R["yT"], np.float32).reshape(cfg.dm, cfg.T)
        shp = (cfg.nps, depth, 2, NH, 128, 128)
        R = {"o_ret": np.asarray(R["o_ret"], np.float32).reshape(shp), "o_gdn": np.asarray(R["o_gdn"], np.float32).reshape(shp),
             "o_hgrn": np.asarray(R["o_hgrn"], np.float32).reshape(shp), "o_lru": np.asarray(R["o_lru"], np.float32)}
        for i in range(cfg.nps):
            yp[cfg.nps * r + i] = yT[:, i * cfg.p_len:(i + 1) * cfg.p_len].T
            nr[cfg.nps * r + i] = R["o_ret"][i]; ng[cfg.nps * r + i] = R["o_gdn"][i]; nhg[cfg.nps * r + i] = R["o_hgrn"][i]
            ol = np.asarray(R["o_lru"], np.float32).reshape(128, cfg.nps, depth, 2, NH)[:, i]
            nl[cfg.nps * r + i] = ol.transpose(1, 2, 3, 0).reshape(depth, 2, NH * 128)
        ys[b, q * cfg.s_len:(q + 1) * cfg.s_len] = yT[:, cfg.S0:].T
    return (yp, ys, nr, nl, ng, nhg)


_CACHE = {}


def kernel(**inputs):
    cfg = Cfg()
    if "nc" not in _CACHE:
        _CACHE["nc"] = build_program(cfg)
    nc = _CACHE["nc"]
    maps = make_in_maps(cfg, inputs)
    res = run_bass_kernel_spmd(nc, maps, core_ids=list(range(cfg.n_cores)))
    return assemble(cfg, res.results)
```
